# Optimizing a Trainium2 kernel written in Bass

```python
import jax, jax.numpy as jnp
from jax import lax
import numpy as np

D_MODEL = 1024
BATCH = 8
SEQ = 2048
DEPTH = 1

MEM_LEN = 256
ATT_HEADS = 8
ATT_KV_HEADS = 2
ATT_HEAD_DIM = 64
ATT_WIDTH = ATT_HEADS * ATT_HEAD_DIM
ATT_KV_WIDTH = ATT_KV_HEADS * ATT_HEAD_DIM
WINDOW = 128
ATT_BLOCK = 128
ROPE_THETA = 500000.0
ROPE_DIM = ATT_HEAD_DIM // 4
HG_HEADS = 4
HG_DIM = 128
HG_WIDTH = HG_HEADS * HG_DIM
HG_CHUNK = 64
MIX_WIDTH = ATT_WIDTH + HG_WIDTH
IN_COLS = ATT_WIDTH + 2 * ATT_KV_WIDTH + 4 * HG_WIDTH
X_HEADS = 4
X_HEAD_DIM = D_MODEL // X_HEADS
FFN_HIDDEN = -(-8 * D_MODEL // (3 * 256)) * 256
RMS_EPS = 1e-6

kernel_name = 'hymba_swa_sink_hgrn2_xattn_layer'


def rms_norm(x, gain):
    xf = x.astype(jnp.float32)
    y = xf * lax.rsqrt(jnp.mean(xf * xf, axis=-1, keepdims=True) + RMS_EPS)
    return (y * gain.astype(jnp.float32)).astype(x.dtype)


def partial_rotary(x, positions):
    half = ROPE_DIM // 2
    inv_freq = jnp.power(jnp.float32(ROPE_THETA), -jnp.arange(half, dtype=jnp.float32) * (2.0 / ROPE_DIM))
    ang = positions.astype(jnp.float32)[..., None] * inv_freq
    cos = jnp.cos(ang)[:, :, None, :]
    sin = jnp.sin(ang)[:, :, None, :]
    xr = x[..., :ROPE_DIM].astype(jnp.float32)
    x1, x2 = xr[..., :half], xr[..., half:]
    rot = jnp.concatenate([x1 * cos - x2 * sin, x2 * cos + x1 * sin], axis=-1).astype(x.dtype)
    return jnp.concatenate([rot, x[..., ROPE_DIM:]], axis=-1)


def sliding_window_sink_attention(q, k, v, sinks):
    B, S = q.shape[0], q.shape[1]
    nb = S // ATT_BLOCK
    G = ATT_HEADS // ATT_KV_HEADS
    qb = q.reshape(B, nb, ATT_BLOCK, ATT_KV_HEADS, G, ATT_HEAD_DIM)
    pad = ((0, 0), (ATT_BLOCK, 0), (0, 0), (0, 0))
    kp = jnp.pad(k, pad).reshape(B, nb + 1, ATT_BLOCK, ATT_KV_HEADS, ATT_HEAD_DIM)
    vp = jnp.pad(v, pad).reshape(B, nb + 1, ATT_BLOCK, ATT_KV_HEADS, ATT_HEAD_DIM)
    kb = jnp.concatenate([kp[:, :-1], kp[:, 1:]], axis=2)
    vb = jnp.concatenate([vp[:, :-1], vp[:, 1:]], axis=2)
    scores = jnp.einsum('bnqhgd,bnkhd->bnhgqk', qb, kb).astype(jnp.float32) * (ATT_HEAD_DIM ** -0.5)
    q_rel = jnp.arange(ATT_BLOCK)[:, None] + ATT_BLOCK
    k_rel = jnp.arange(2 * ATT_BLOCK)[None, :]
    diff = q_rel - k_rel
    band = (diff >= 0) & (diff < WINDOW)
    key_abs = jnp.arange(nb)[:, None] * ATT_BLOCK - ATT_BLOCK + jnp.arange(2 * ATT_BLOCK)[None, :]
    valid = band[None] & (key_abs >= 0)[:, None, :]
    scores = jnp.where(valid[None, :, None, None], scores, -jnp.inf)
    sink = jnp.broadcast_to(sinks.astype(jnp.float32).reshape(1, 1, ATT_KV_HEADS, G, 1, 1),
                            scores.shape[:-1] + (1,))
    probs = jax.nn.softmax(jnp.concatenate([scores, sink], axis=-1), axis=-1)[..., :-1]
    out = jnp.einsum('bnhgqk,bnkhd->bnqhgd', probs.astype(v.dtype), vb)
    return out.reshape(B, S, ATT_WIDTH)


def hgrn2_chunkwise(q, f_logit, i, lb):
    B, S = q.shape[0], q.shape[1]
    nc = S // HG_CHUNK
    lb = lb.reshape(HG_HEADS, HG_DIM).astype(jnp.float32)
    f = lb + (1.0 - lb) * jax.nn.sigmoid(f_logit.astype(jnp.float32))
    log_f = jnp.log(f)
    key = 1.0 - f
    qf = jax.nn.silu(q.astype(jnp.float32))
    vf = i.astype(jnp.float32)

    def to_chunks(t):
        return t.reshape(B, nc, HG_CHUNK, HG_HEADS, t.shape[-1]).transpose(1, 0, 3, 2, 4)

    causal = jnp.tril(jnp.ones((HG_CHUNK, HG_CHUNK), dtype=bool))

    def step(state, inp):
        qc, kc, vc, gc = inp
        b = jnp.cumsum(gc, axis=2)
        o_inter = jnp.einsum('bhtd,bhde->bhte', qc * jnp.exp(b), state)
        rel = b[:, :, :, None, :] - b[:, :, None, :, :]
        decay = jnp.exp(jnp.where(causal[:, :, None], rel, -jnp.inf))
        scores = jnp.einsum('bhtd,bhsd,bhtsd->bhts', qc, kc, decay)
        o_intra = jnp.einsum('bhts,bhse->bhte', scores, vc)
        b_last = b[:, :, -1:, :]
        k_dec = kc * jnp.exp(b_last - b)
        new_state = jnp.exp(b_last[:, :, 0, :])[..., None] * state + jnp.einsum('bhsd,bhse->bhde', k_dec, vc)
        return new_state, o_inter + o_intra

    s0 = jnp.zeros((B, HG_HEADS, HG_DIM, HG_DIM), jnp.float32)
    _, o = lax.scan(step, s0, (to_chunks(qf), to_chunks(key), to_chunks(vf), to_chunks(log_f)))
    return o.transpose(1, 0, 3, 2, 4).reshape(B, S, HG_HEADS, HG_DIM)


def setup_inputs(seed: int = 0) -> dict:
    key = jax.random.key(seed)
    ks = jax.random.split(key, 20)
    f32 = jnp.float32

    def w(k, shape, fan_in):
        return jax.random.normal(k, shape, f32) * (fan_in ** -0.5)

    def gain(k, shape):
        return 1.0 + 0.02 * jax.random.normal(k, shape, f32)

    x = jax.random.normal(ks[0], (BATCH, SEQ, D_MODEL), f32)
    mem = jax.random.normal(ks[1], (BATCH, MEM_LEN, D_MODEL), f32)
    offset = jax.random.randint(ks[2], (BATCH, 1), 0, 4096, dtype=jnp.int32)
    positions = offset + jnp.arange(SEQ, dtype=jnp.int32)[None, :]
    return {
        'x': x,
        'mem': mem,
        'positions': positions,
        'norm_mix': gain(ks[3], (DEPTH, D_MODEL)),
        'w_in': w(ks[4], (DEPTH, D_MODEL, IN_COLS), D_MODEL),
        'att_sinks': 0.5 * jax.random.normal(ks[5], (DEPTH, ATT_HEADS), f32),
        'att_out_gain': gain(ks[6], (DEPTH, ATT_WIDTH)),
        'hg_lb_logits': 0.5 * jax.random.normal(ks[7], (DEPTH + 1, HG_WIDTH), f32),
        'hg_out_gain': gain(ks[8], (DEPTH, HG_WIDTH)),
        'w_out': w(ks[9], (DEPTH, MIX_WIDTH, D_MODEL), MIX_WIDTH),
        'norm_xattn': gain(ks[10], (DEPTH, D_MODEL)),
        'norm_mem': gain(ks[11], (DEPTH, D_MODEL)),
        'w_xq': w(ks[12], (DEPTH, D_MODEL, D_MODEL), D_MODEL),
        'w_xkv': w(ks[13], (DEPTH, D_MODEL, 2 * D_MODEL), D_MODEL),
        'w_xo': w(ks[14], (DEPTH, D_MODEL, D_MODEL), D_MODEL),
        'norm_ffn': gain(ks[15], (DEPTH, D_MODEL)),
        'w_gate_up': w(ks[16], (DEPTH, D_MODEL, 2 * FFN_HIDDEN), D_MODEL),
        'w_down': w(ks[17], (DEPTH, FFN_HIDDEN, D_MODEL), FFN_HIDDEN),
        'norm_final': gain(ks[18], (D_MODEL,)),
    }


def reference(x, mem, positions, norm_mix, w_in, att_sinks, att_out_gain, hg_lb_logits,
              hg_out_gain, w_out, norm_xattn, norm_mem, w_xq, w_xkv, w_xo, norm_ffn,
              w_gate_up, w_down, norm_final):
    B, S = x.shape[0], x.shape[1]
    M = mem.shape[1]
    lower_bounds = jnp.cumsum(jax.nn.softmax(hg_lb_logits.astype(jnp.float32), axis=0), axis=0)
    split_at = np.cumsum([ATT_WIDTH, ATT_KV_WIDTH, ATT_KV_WIDTH, HG_WIDTH, HG_WIDTH, HG_WIDTH]).tolist()
    for l in range(DEPTH):
        h = rms_norm(x, norm_mix[l])
        proj = h @ w_in[l]
        q_a, k_a, v_a, q_r, f_r, i_r, g_r = jnp.split(proj, split_at, axis=-1)
        q_a = partial_rotary(q_a.reshape(B, S, ATT_HEADS, ATT_HEAD_DIM), positions)
        k_a = partial_rotary(k_a.reshape(B, S, ATT_KV_HEADS, ATT_HEAD_DIM), positions)
        v_a = v_a.reshape(B, S, ATT_KV_HEADS, ATT_HEAD_DIM)
        att = sliding_window_sink_attention(q_a, k_a, v_a, att_sinks[l])
        att = rms_norm(att, att_out_gain[l])

        rec = hgrn2_chunkwise(q_r.reshape(B, S, HG_HEADS, HG_DIM),
                              f_r.reshape(B, S, HG_HEADS, HG_DIM),
                              i_r.reshape(B, S, HG_HEADS, HG_DIM),
                              lower_bounds[l])
        rec = rec * lax.rsqrt(jnp.mean(rec * rec, axis=-1, keepdims=True) + RMS_EPS)
        rec = rec.reshape(B, S, HG_WIDTH) * hg_out_gain[l].astype(jnp.float32)
        rec = (rec * jax.nn.silu(g_r.astype(jnp.float32))).astype(x.dtype)

        x = x + jnp.concatenate([att, rec], axis=-1) @ w_out[l]

        hq = rms_norm(x, norm_xattn[l])
        mn = rms_norm(mem, norm_mem[l])
        xq = (hq @ w_xq[l]).reshape(B, S, X_HEADS, X_HEAD_DIM)
        xk, xv = jnp.split(mn @ w_xkv[l], 2, axis=-1)
        xk = xk.reshape(B, M, X_HEADS, X_HEAD_DIM)
        xv = xv.reshape(B, M, X_HEADS, X_HEAD_DIM)
        xs = jnp.einsum('bqhd,bkhd->bhqk', xq, xk).astype(jnp.float32) * (X_HEAD_DIM ** -0.5)
        xp = jax.nn.softmax(xs, axis=-1).astype(xv.dtype)
        xo = jnp.einsum('bhqk,bkhd->bqhd', xp, xv).reshape(B, S, D_MODEL)
        x = x + xo @ w_xo[l]

        hf = rms_norm(x, norm_ffn[l])
        gate, up = jnp.split(hf @ w_gate_up[l], 2, axis=-1)
        x = x + (jax.nn.silu(gate) * up) @ w_down[l]
    return rms_norm(x, norm_final)
```

```python
import numpy as np
import concourse.bass as bass
import concourse.mybir as mybir
from concourse.bass_utils import run_bass_kernel_spmd

F32 = mybir.dt.float32
BF16 = mybir.dt.bfloat16
I32 = mybir.dt.int32
AF = mybir.ActivationFunctionType
ALU = mybir.AluOpType

S = 2048
DM = 1024
G = 512
NGRP = 4
MEM = 256
NS = 9
EPS = 1e-6
NPRM = 64
THIRDS = [list(range(0, 8)), list(range(8, 15)), list(range(15, 22))]

U_XK = 0
U_XV = 8
U_QA = 16
U_KA = 20
U_VA = 21
U_HG = 22
U_WO = 38
U_XQ = 46
U_XO = 54
U_FFN = 62


def _ffn_units():
    lst = []
    for t, J in enumerate(THIRDS):
        for j in J:
            lst.append(("gate", t, j))
            lst.append(("up", t, j))
        for m in range(8):
            lst.append(("down", t, m))
    return lst


FFN_UNITS = _ffn_units()
NU = U_FFN + len(FFN_UNITS)


def _unit_cols(u):
    if u >= U_FFN:
        kind, t, _ = FFN_UNITS[u - U_FFN]
        if kind == "down":
            return len(THIRDS[t]) * 128
    return 1024


class _Eng:
    def __init__(self, name, eng, sem):
        self.name = name
        self.eng = eng
        self.sem = sem
        self.count = 0
        self.seen = {}
        self.pending = False


class Ctx:
    def __init__(self, nc):
        self.nc = nc
        self.E = {}
        for name, e in (("pe", nc.tensor), ("act", nc.scalar), ("dve", nc.vector),
                        ("pool", nc.gpsimd), ("sp", nc.sync)):
            self.E[name] = _Eng(name, e, nc.alloc_semaphore("s_" + name))
        self.keys = {}
        self.dsem = {}

    def _deps(self, reads, writes):
        need = {}

        def add(tok):
            sid, sem, val = tok
            if sid not in need or need[sid][1] < val:
                need[sid] = (sem, val)

        for k in reads:
            st = self.keys.get(k)
            if st is not None and st[0] is not None:
                add(st[0])
        for k in writes:
            st = self.keys.get(k)
            if st is not None:
                if st[0] is not None:
                    add(st[0])
                for t in st[1].values():
                    add(t)
        return need

    def _wait(self, es, need):
        for sid, (sem, val) in need.items():
            if sid.startswith("d_"):
                val = self.dsem[sid[2:]][1]
            if es.seen.get(sid, 0) >= val:
                continue
            if sid == es.name and es.name == "pe":
                continue
            if sid == es.name and es.pending and val > es.count:
                raise RuntimeError("wait on own pending instruction")
            es.eng.wait_ge(sem, val)
            es.seen[sid] = val

    def _record(self, tok, reads, writes):
        for k in reads:
            st = self.keys.setdefault(k, [None, {}])
            old = st[1].get(tok[0])
            if old is None or old[2] < tok[2]:
                st[1][tok[0]] = tok
        for k in writes:
            self.keys[k] = [tok, {}]

    def op(self, engname, fn, reads=(), writes=(), inc=True):
        inc = True
        psr = [k for k in reads if isinstance(k, tuple) and k[0] == "ps"]
        if psr:
            reads = [k for k in reads if k not in psr]
            writes = list(writes) + psr
        es = self.E[engname]
        self._wait(es, self._deps(reads, writes))
        ins = fn(es.eng)
        if inc:
            es.count += 1
            ins.then_inc(es.sem, 1)
            tok = (es.name, es.sem, es.count)
            es.pending = False
        else:
            tok = (es.name, es.sem, es.count + 1)
            es.pending = True
        self._record(tok, reads, writes)
        return tok

    def dma(self, queue, semname, out, in_, reads=(), writes=()):
        es = self.E[queue]
        self._wait(es, self._deps(reads, writes))
        d = self.dsem.get(semname)
        if d is None:
            d = [self.nc.alloc_semaphore("d_" + semname), 0]
            self.dsem[semname] = d
        d[1] += 16
        es.eng.dma_start(out=out, in_=in_).then_inc(d[0], 16)
        tok = ("d_" + semname, d[0], d[1])
        self._record(tok, reads, writes)
        return tok

    def alias(self, newkeys, oldkeys):
        toks = {}
        for k in oldkeys:
            st = self.keys.get(k)
            if st is None:
                continue
            cand = list(st[1].values())
            if st[0] is not None:
                cand.append(st[0])
            for t in cand:
                if t[0] not in toks or toks[t[0]][2] < t[2]:
                    toks[t[0]] = t
        for k in newkeys:
            st = self.keys.setdefault(k, [None, {}])
            for sid, t in toks.items():
                if sid not in st[1] or st[1][sid][2] < t[2]:
                    st[1][sid] = t


class _Stop(Exception):
    pass


def build(dbg=None, stop=None):
    nc = bass.Bass("TRN2", target_bir_lowering=False)
    C = Ctx(nc)
    dbg = dbg or []
    dbg_out = {}

    xT = nc.dram_tensor("xT", [DM, S], F32, kind="ExternalInput").ap()
    memT = nc.dram_tensor("memT", [DM, MEM], F32, kind="ExternalInput").ap()
    pos = nc.dram_tensor("pos", [128, S], I32, kind="ExternalInput").ap()
    wu = nc.dram_tensor("wu", [NU, 128, 1024], F32, kind="ExternalInput").ap()
    prm = nc.dram_tensor("prm", [128, NPRM], F32, kind="ExternalInput").ap()
    cst = nc.dram_tensor("cst", [128, 6 * 128], F32, kind="ExternalInput").ap()
    yT = nc.dram_tensor("yT", [DM, S], F32, kind="ExternalOutput").ap()
    yv = yT.rearrange("(c p) t -> p c t", p=128)

    X = nc.alloc_sbuf_tensor("X", [128, 8, S], F32)
    HT = nc.alloc_sbuf_tensor("HT", [128, 8, S], BF16)
    BIG = nc.alloc_sbuf_tensor("BIG", [128, 8, S], BF16)
    RING = nc.alloc_sbuf_tensor("RING", [128, NS, 1024], BF16)
    XKT = nc.alloc_sbuf_tensor("XKT", [128, 8, MEM], BF16)
    XV = nc.alloc_sbuf_tensor("XV", [128, 2, DM], BF16)
    PRM = nc.alloc_sbuf_tensor("PRM", [128, NPRM], F32)
    CB = nc.alloc_sbuf_tensor("CB", [128, 6, 128], BF16)
    LBT = nc.alloc_sbuf_tensor("LBT", [128, 20], F32)
    NSF = 10
    NSB = 11
    SF = [nc.alloc_sbuf_tensor("SF%d" % i, [128, G], F32) for i in range(NSF)]
    SB = [nc.alloc_sbuf_tensor("SB%d" % i, [128, G], BF16) for i in range(NSB)]
    SWAX = nc.alloc_sbuf_tensor("SWAX", [128, 3 * S], BF16)
    QA = SWAX[:, 0:S].rearrange("p (a b) -> p a b", a=4)
    KA = SWAX[:, S:2 * S]
    VA = SWAX[:, 2 * S:3 * S].rearrange("p (a b) -> p a b", a=16)
    XQ = SWAX[:, 0:2 * S].rearrange("p (a b) -> p a b", a=2)
    SGB = nc.alloc_sbuf_tensor("SGB", [128, 2, 8, 128], BF16)
    ZB = nc.alloc_sbuf_tensor("ZB", [128, 9, 128], F32)
    EBL = nc.alloc_sbuf_tensor("EBL", [128, 2, 8], F32)
    ES = nc.alloc_sbuf_tensor("ES", [128, 4], F32)
    LN2C = nc.alloc_sbuf_tensor("LN2C", [128, 1], F32)
    PS = [nc.alloc_psum_tensor("PS%d" % i, [128, G], F32) for i in range(7)]
    PST = nc.alloc_psum_tensor("PST", [128, 2 * G], BF16)

    IDENT = CB[:, 0, :]
    ONES = CB[:, 1, :]
    PSW = CB[:, 2, :]
    MCUR = CB[:, 3, :]
    MPREV = CB[:, 4, :]
    MBLK = CB[:, 5, :]

    def psk(i):
        return ("ps", i)

    def sfk(i):
        return ("sf", i)

    def sbk(i):
        return ("sb", i)

    class WS:
        issued = 0
        rel = 0

    def w_pump():
        while WS.issued < min(NU, WS.rel + NS):
            u = WS.issued
            sl = u % NS
            ncol = _unit_cols(u)
            C.dma("pool", "ring%d" % sl, RING[:, sl, 0:ncol], wu[u, :, 0:ncol], writes=[("ring", sl)])
            WS.issued += 1

    def w_slot(u):
        assert u < WS.issued and u >= WS.rel, (u, WS.issued, WS.rel)
        return RING[:, u % NS, :], ("ring", u % NS)

    def w_release(upto):
        if upto > WS.rel:
            WS.rel = upto
        w_pump()

    def mm(out, lhsT, rhs, start, stop, reads, writes, inc=None):
        if inc is None:
            inc = stop
        return C.op("pe", lambda e: e.matmul(out, lhsT=lhsT, rhs=rhs, start=start, stop=stop),
                    reads=reads, writes=writes, inc=inc)

    def act(out, in_, func, reads, writes, scale=1.0, bias=0.0):
        return C.op("act", lambda e: e.activation(out=out, in_=in_, func=func, scale=scale, bias=bias),
                    reads=reads, writes=writes)

    def acopy(out, in_, reads, writes):
        return C.op("act", lambda e: e.copy(out=out, in_=in_), reads=reads, writes=writes)

    def vtt(out, in0, in1, op, reads, writes, eng="dve"):
        return C.op(eng, lambda e: e.tensor_tensor(out=out, in0=in0, in1=in1, op=op), reads=reads, writes=writes)

    def vts(out, in0, s1, s2, op0, op1, reads, writes, eng="dve"):
        if s2 is None:
            return C.op(eng, lambda e: e.tensor_scalar(out=out, in0=in0, scalar1=s1, scalar2=None, op0=op0),
                        reads=reads, writes=writes)
        return C.op(eng, lambda e: e.tensor_scalar(out=out, in0=in0, scalar1=s1, scalar2=s2, op0=op0, op1=op1),
                    reads=reads, writes=writes)

    def vstt(out, in0, scalar, in1, op0, op1, reads, writes):
        return C.op("dve", lambda e: e.scalar_tensor_tensor(out=out, in0=in0, scalar=scalar, in1=in1, op0=op0, op1=op1),
                    reads=reads, writes=writes)

    def vcopy(out, in_, reads, writes, eng="dve"):
        return C.op(eng, lambda e: e.tensor_copy(out=out, in_=in_), reads=reads, writes=writes)

    def vrecip(out, in_, reads, writes):
        return C.op("dve", lambda e: e.reciprocal(out=out, in_=in_), reads=reads, writes=writes)

    def rsqrt_from(ps_ap, pskey, sf_i, ncol, inv_n):
        r = SF[sf_i][:, 0:ncol]
        act(r, ps_ap, AF.Ln, [pskey], [sfk(sf_i)], scale=inv_n, bias=EPS)
        act(r, r, AF.Exp, [sfk(sf_i)], [sfk(sf_i)], scale=-0.5)
        return r

    def tap(name, ap, shape, keys, dtype=F32):
        if name not in dbg:
            return
        d = nc.dram_tensor("dbg_" + name, list(shape), F32, kind="ExternalOutput").ap()
        dbg_out[name] = d
        C.dma("pool" if dtype != F32 else "sp", "dbg", d, ap, reads=keys)

    def stage(name):
        if stop == name:
            raise _Stop()

    try:
        C.dma("sp", "prm", PRM[:, :], prm[:, :], writes=["PRM"])
        C.dma("pool", "cst", CB[:, :, :].rearrange("p a b -> p (a b)"), cst[:, :], writes=["CB"])
        w_pump()
        MEMX = BIG[:, 0:2, :].rearrange("p a b -> p (a b)").bitcast(F32).rearrange("p (c t) -> p c t", c=8)
        MN = BIG[:, 2, 0:2048].rearrange("p (c t) -> p c t", c=8)
        for c in range(8):
            C.dma("sp", "mem", MEMX[:, c, :], memT[c * 128:(c + 1) * 128, :], writes=[("MEMX", c)])
        xv = xT.rearrange("(c p) t -> p c t", p=128)
        for g in range(NGRP):
            for h4 in range(2):
                C.dma("sp", "x%d" % g, X[:, 4 * h4:4 * h4 + 4, g * G:(g + 1) * G], xv[:, 4 * h4:4 * h4 + 4, g * G:(g + 1) * G],
                      writes=[("X", c, g) for c in range(4 * h4, 4 * h4 + 4)])

        vtt(LBT[:, 0:4], PRM[:, 48:52], PRM[:, 52:56], ALU.subtract, ["PRM"], ["LB0"])
        act(LBT[:, 0:4], LBT[:, 0:4], AF.Tanh, ["LB0"], ["LB0"], scale=0.5)
        vts(LBT[:, 4:8], LBT[:, 0:4], -0.25, 0.25, ALU.mult, ALU.add, ["LB0"], ["LB1"])
        vts(LBT[:, 8:12], LBT[:, 0:4], 0.25, 0.75, ALU.mult, ALU.add, ["LB0"], ["LB2"])
        vts(LBT[:, 12:16], LBT[:, 0:4], 0.25, -0.25, ALU.mult, ALU.add, ["LB0"], ["LB3"])
        vts(LBT[:, 16:20], LBT[:, 0:4], -0.25, 0.25, ALU.mult, ALU.add, ["LB0"], ["LB4"])
        act(ES[:, :], PRM[:, 58:62], AF.Exp, ["PRM"], ["ES"])
        C.op("dve", lambda e: e.memset(LN2C[:, :], 20.79441541679836), writes=["LN2C"])

        stage("setup")
        def norm_T(src, srckey, gcol, dst, dstkey, t0, ncol, gi):
            ss = PS[6][:, 0:ncol]
            for c in range(8):
                sq = SB[6 + (c % 2)][:, 0:ncol]
                act(sq, src[:, c, t0:t0 + ncol], AF.Square, [srckey(c, gi)], [sbk(6 + (c % 2))])
                mm(ss, ONES, sq, c == 0, c == 7, [sbk(6 + (c % 2)), "CB"], [psk(6)])
            r = rsqrt_from(ss, psk(6), 7, ncol, 1.0 / DM)
            for c in range(8):
                vstt(dst[:, c, t0:t0 + ncol], src[:, c, t0:t0 + ncol], PRM[:, gcol + c:gcol + c + 1], r,
                     ALU.mult, ALU.mult, [srckey(c, gi), sfk(7), "PRM"], [dstkey(c, gi)])

        xkey = lambda c, g: ("X", c, g)
        hkey = lambda c, g: ("HT", c, g)

        norm_T(MEMX, lambda c, g: ("MEMX", c), 16, MN, lambda c, g: ("MN", c), 0, MEM, 0)
        mnkeys = [("MN", c) for c in range(8)]
        tap("mn", MN, [128, 8, MEM], mnkeys, BF16)
        stage("mn")
        for m in range(8):
            sl, sk = w_slot(U_XK + m)
            pb = m % 2
            for c in range(8):
                mm(PS[pb][:, 0:MEM], sl[:, c * 128:(c + 1) * 128], MN[:, c, :], c == 0, c == 7,
                   [sk, ("MN", c)], [psk(pb)])
            acopy(XKT[:, m, :], PS[pb][:, 0:MEM], [psk(pb)], [("XKT", m)])
            w_release(U_XK + m + 1)
        for hf in range(2):
            for tt in range(2):
                pb = 2 + tt
                for c in range(8):
                    sl, sk = w_slot(U_XV + hf * 4 + c // 2)
                    mm(PS[pb][:, :], MN[:, c, tt * 128:(tt + 1) * 128], sl[:, (c % 2) * 512:(c % 2 + 1) * 512],
                       c == 0, c == 7, [sk, ("MN", c)], [psk(pb)])
                acopy(XV[:, tt, hf * 512:(hf + 1) * 512], PS[pb][:, :], [psk(pb)], [("XV", tt, hf)])
            w_release(U_XV + hf * 4 + 4)
        xvkeys = [("XV", tt, hf) for tt in range(2) for hf in range(2)]
        tap("xkt", XKT[:, :, :], [128, 8, MEM], [("XKT", m) for m in range(8)], BF16)
        tap("xv", XV[:, :, :], [128, 2, DM], xvkeys, BF16)
        stage("xv")

        for g in range(NGRP):
            norm_T(X, xkey, 0, HT, hkey, g * G, G, g)
        tap("ht", HT[:, :, :], [128, 8, S], [hkey(c, g) for c in range(8) for g in range(NGRP)], BF16)
        stage("ht")

        MIX = BIG
        mixkey = lambda c, g: ("MIX", c, g)
        C.alias([mixkey(c, g) for c in range(8) for g in range(NGRP)],
                [("MEMX", c) for c in range(8)] + mnkeys)

        def proj_T(u, pb, g):
            sl, sk = w_slot(u)
            for c in range(8):
                mm(PS[pb][:, :], sl[:, c * 128:(c + 1) * 128], HT[:, c, g * G:(g + 1) * G], c == 0, c == 7,
                   [sk, hkey(c, g)], [psk(pb)])

        TWO_PI = 6.283185

        def qakey(bi, i):
            return ("QA", bi, i)

        def swa_tables(g):
            t0 = g * G
            ci, si = 0, 1
            posi = SF[2][:, :].bitcast(I32)
            C.dma("sp", "pos", posi, pos[:, t0:t0 + G], writes=[sfk(2)])
            vcopy(SF[3][:, :], posi, [sfk(2)], [sfk(3)])
            vts(SF[3][:, :], SF[3][:, :], PRM[:, 56:57], None, ALU.mult, None, [sfk(3), "PRM"], [sfk(3)])
            ni = SF[2][:, :].bitcast(I32)
            vcopy(ni, SF[3][:, :], [sfk(3)], [sfk(2)])
            vcopy(SF[4][:, :], ni, [sfk(2)], [sfk(4)])
            vtt(SF[4][:, :], SF[3][:, :], SF[4][:, :], ALU.subtract, [sfk(3), sfk(4)], [sfk(4)])
            act(SF[si][:, :], SF[4][:, :], AF.Sin, [sfk(4), "PRM"], [sfk(si)], scale=PRM[:, 57:58])
            vts(SF[3][:, :], SF[3][:, :], 0.25, None, ALU.add, None, [sfk(3)], [sfk(3)])
            vcopy(ni, SF[3][:, :], [sfk(3)], [sfk(2)])
            vcopy(SF[4][:, :], ni, [sfk(2)], [sfk(4)])
            vtt(SF[4][:, :], SF[3][:, :], SF[4][:, :], ALU.subtract, [sfk(3), sfk(4)], [sfk(4)])
            act(SF[ci][:, :], SF[4][:, :], AF.Sin, [sfk(4)], [sfk(ci)], scale=TWO_PI)

        def swa_proj(g, i):
            t0 = g * G
            ci, si = 0, 1
            pb = 2 + 2 * (i % 2)
            proj_T(U_QA + i, pb, g)
            acopy(SB[7 + (i % 2)][:, :], PS[pb][:, :], [psk(pb)], [sbk(7 + (i % 2))])
            mm(PS[pb + 1][:, :], PSW, SB[7 + (i % 2)][:, :], True, True, [sbk(7 + (i % 2)), "CB"], [psk(pb + 1)])
            t1, t2 = SF[2 + 2 * (i % 2)], SF[3 + 2 * (i % 2)]
            k1, k2 = sfk(2 + 2 * (i % 2)), sfk(3 + 2 * (i % 2))
            vtt(t1[:, :], PS[pb][:, :], SF[ci][:, :], ALU.mult, [psk(pb), sfk(ci)], [k1])
            vtt(t2[:, :], PS[pb + 1][:, :], SF[si][:, :], ALU.mult, [psk(pb + 1), sfk(si)], [k2])
            if i < 4:
                vtt(QA[:, i, :], t1[:, :], t2[:, :], ALU.add, [k1, k2], [qakey(0, i)])
            else:
                vtt(KA[:, t0:t0 + G], t1[:, :], t2[:, :], ALU.add, [k1, k2], [("KA", g)])

        def swa_v(g):
            t0 = g * G
            sl, sk = w_slot(U_VA)
            for tt in range(4):
                for c in range(8):
                    mm(PS[6][:, tt * 128:(tt + 1) * 128], HT[:, c, t0 + tt * 128:t0 + (tt + 1) * 128],
                       sl[:, c * 128:(c + 1) * 128], c == 0, c == 7, [sk, hkey(c, g)], [psk(6)])
            acopy(VA[:, 4 * g:4 * g + 4, :].rearrange("p a b -> p (a b)"), PS[6][:, :], [psk(6)], [("VA", g)])

        PT_IDX = [[0, 1, 2, 3], [7, 8, 9, 10]]
        OD_BANK = [(2, 3), (4, 5)]
        TAIL_SF = [(7, 8), (5, 6)]

        def swa_front_parts(n):
            g, j = divmod(n, 4)
            st = n % 2
            q0 = j * 128
            kts = [1] if n == 0 else [0, 1]
            ob, db = OD_BANK[st]
            pt = {}
            combos = [(kt, gg) for kt in kts for gg in range(2)]

            def sc(idx):
                kt, gg = combos[idx]
                pr = slice(gg * 64, (gg + 1) * 64)
                kb = n - 1 + kt
                sb_i = idx % 2
                mm(PS[sb_i][:, :].rearrange("p (a b) -> p a b", a=4), KA[pr, kb * 128:(kb + 1) * 128],
                   QA[pr, :, q0:q0 + 128], True, True,
                   [("KA", kb // 4)] + [qakey(0, i) for i in range(4)], [psk(sb_i)])
                et = SB[4 + (idx % 2)]
                act(et[:, :], PS[sb_i][:, :], AF.Exp, [psk(sb_i)], [sbk(4 + (idx % 2))], scale=0.125)
                msk = MCUR if kt == 1 else MPREV
                pi = PT_IDX[st][idx]
                vtt(SB[pi][:, :].rearrange("p (a b) -> p a b", a=4), et[:, :].rearrange("p (a b) -> p a b", a=4),
                    msk.unsqueeze(1).broadcast_to([128, 4, 128]), ALU.mult,
                    [sbk(4 + (idx % 2)), "CB"], [sbk(pi)])
                pt[(gg, kt)] = pi

            def part1():
                for idx in range(min(2, len(combos))):
                    sc(idx)

            def part2():
                for idx in range(2, len(combos)):
                    sc(idx)

            def part3():
                for ii, kt in enumerate(kts):
                    kb = n - 1 + kt
                    for gg in range(2):
                        pr = slice(gg * 64, (gg + 1) * 64)
                        mm(PS[ob][pr, :], VA[:, kb, gg * 64:(gg + 1) * 64], SB[pt[(gg, kt)]][:, :], ii == 0, ii == len(kts) - 1,
                           [("VA", kb // 4), sbk(pt[(gg, kt)])], [psk(ob)])
                for ii, kt in enumerate(kts):
                    for gg in range(2):
                        pr = slice(gg * 64, (gg + 1) * 64)
                        mm(PS[db][pr, :], ONES[:, 0:64], SB[pt[(gg, kt)]][:, :], ii == 0, ii == len(kts) - 1,
                           ["CB", sbk(pt[(gg, kt)])], [psk(db)])

            return part1, part2, part3

        def swa_tail_parts(n):
            g, j = divmod(n, 4)
            st = n % 2
            t0 = g * G
            q0 = j * 128
            ob, db = OD_BANK[st]
            fd, fa = TAIL_SF[st]

            def part1():
                for hh in range(4):
                    act(SF[fd][:, hh * 128:(hh + 1) * 128], PS[db][:, hh * 128:(hh + 1) * 128], AF.Ln, [psk(db), "ES"], [sfk(fd)],
                        bias=ES[:, hh:hh + 1])
                act(SF[fd][:, :], SF[fd][:, :], AF.Exp, [sfk(fd)], [sfk(fd)], scale=-1.0)
                vtt(SF[fa][:, :], PS[ob][:, :], SF[fd][:, :], ALU.mult, [psk(ob), sfk(fd)], [sfk(fa)])

            def part2():
                act(SB[6][:, :], SF[fa][:, :], AF.Square, [sfk(fa)], [sbk(6)])
                for hh in range(4):
                    mm(PS[6][:, 0:128], ONES, SB[6][:, hh * 128:(hh + 1) * 128], hh == 0, hh == 3, [sbk(6), "CB"], [psk(6)])

            def part3():
                r = rsqrt_from(PS[6][:, 0:128], psk(6), 9, 128, 1.0 / 512)
                for hh in range(4):
                    vstt(MIX[:, hh, t0 + q0:t0 + q0 + 128], SF[fa][:, hh * 128:(hh + 1) * 128], PRM[:, 40 + hh:41 + hh], r,
                         ALU.mult, ALU.mult, [sfk(fa), sfk(9), "PRM"], [mixkey(hh, g)])

            return part1, part2, part3

        def swa_blocks(n0):
            F = [swa_front_parts(n0 + k) for k in range(4)]
            T = [swa_tail_parts(n0 + k) for k in range(4)]
            for k in range(2):
                for f in F[k]:
                    f()
            for k in range(4):
                nf = F[k + 2] if k + 2 < 4 else (lambda: None, lambda: None, lambda: None)
                T[k][0]()
                nf[0]()
                T[k][1]()
                nf[1]()
                T[k][2]()
                nf[2]()

        for g in range(NGRP):
            swa_tables(g)
            for i in range(5):
                swa_proj(g, i)
            swa_v(g)
            if g == 0:
                tap("qa0", QA[:, :, :], [128, 4, G], [qakey(0, i) for i in range(4)], BF16)
            n0 = 4 * g
            swa_blocks(n0)
        w_release(U_VA + 1)
        tap("mixa", MIX[:, 0:4, :], [128, 4, S], [mixkey(c, g) for c in range(4) for g in range(NGRP)], BF16)
        stage("mixa")

        QTb = [SB[0], SB[1]]
        KTb = [SB[2], SB[3]]
        KDb = [SB[4], SB[5]]
        VRb = [SB[6], SB[7]]
        I_KDT, I_AM, I_SQO = 8, 9, 10
        SGTb = [SF[5], SF[6]]

        def hg_ctx(hd, g, bi):
            d = dict(hd=hd, g=g, bi=bi, t0=g * G, par=g % 2,
                     qt=QTb[bi], kt=KTb[bi], kd=KDb[bi], vr=VRb[bi],
                     kq=sbk(bi), kk=sbk(2 + bi), kkd=sbk(4 + bi), kv=sbk(6 + bi))
            d["uf"], d["uq"], d["ug"], d["ui"] = (U_HG + 4 * hd + k for k in range(4))
            d["ik"] = 0 if bi == 0 else 9
            return d

        def hgA_F(x):
            hd, g, bi, ik = x["hd"], x["g"], x["bi"], x["ik"]
            proj_T(x["uf"], 0, g)
            act(SF[ik][:, :], PS[0][:, :], AF.Tanh, [psk(0)], [sfk(ik)], scale=0.5)
            act(SF[3][:, :], SF[ik][:, :], AF.Identity, [sfk(ik), "LB1", "LB2"], [sfk(3)],
                scale=LBT[:, 4 + hd:5 + hd], bias=LBT[:, 8 + hd:9 + hd])
            act(SF[ik][:, :], SF[ik][:, :], AF.Identity, [sfk(ik), "LB3", "LB4"], [sfk(ik)],
                scale=LBT[:, 12 + hd:13 + hd], bias=LBT[:, 16 + hd:17 + hd])

        def hgA_Fd(x, part="all"):
            bi = x["bi"]
            P3 = SF[1][:, :].rearrange("p (c s) -> p c s", s=64)
            if part in ("all", "dve"):
                F3 = SF[3][:, :].rearrange("p (c s) -> p c s", s=64)
                Z3 = SF[8][:, :].rearrange("p (c s) -> p c s", s=64)
                vcopy(Z3[:, :, 0], F3[:, :, 0], [sfk(3)], [sfk(8)])
                C.op("dve", lambda e: e.tensor_tensor_scan(out=SF[1][:, :], data0=SF[3][:, :], data1=SF[8][:, :], initial=1.0,
                                                           op0=ALU.mult, op1=ALU.max), reads=[sfk(3), sfk(8)], writes=[sfk(1)])
                vcopy(EBL[:, bi, :], P3[:, :, 63], [sfk(1)], [("EBL", bi)])
            if part in ("all", "act"):
                act(SF[2][:, :], SF[1][:, :], AF.Ln, [sfk(1)], [sfk(2)], scale=float(2 ** 30))
                act(SF[2][:, :], SF[2][:, :], AF.Exp, [sfk(2), "LN2C"], [sfk(2)], scale=-1.0, bias=LN2C[:, 0:1])

        def hgA_Q(x):
            g = x["g"]
            P3 = SF[1][:, :].rearrange("p (c s) -> p c s", s=64)
            proj_T(x["uq"], 1, g)
            act(SF[4][:, :], PS[1][:, :], AF.Silu, [psk(1)], [sfk(4)])
            vtt(x["qt"][:, :], SF[4][:, :], SF[1][:, :], ALU.mult, [sfk(4), sfk(1)], [x["kq"]], eng="pool")
            vtt(x["kt"][:, :], SF[x["ik"]][:, :], SF[2][:, :], ALU.mult, [sfk(x["ik"]), sfk(2)], [x["kk"]], eng="pool")

        def hgA_G(x):
            proj_T(x["ug"], 2, x["g"])
            act(SGTb[x["bi"]][:, :], PS[2][:, :], AF.Silu, [psk(2)], [sfk(5 + x["bi"])])

        def hgA_V(x):
            g, t0 = x["g"], x["t0"]
            sl, sk = w_slot(x["ui"])
            for tt in range(4):
                for c in range(8):
                    mm(PS[3][:, tt * 128:(tt + 1) * 128], HT[:, c, t0 + tt * 128:t0 + (tt + 1) * 128],
                       sl[:, c * 128:(c + 1) * 128], c == 0, c == 7, [sk, hkey(c, g)], [psk(3)])
            acopy(x["vr"][:, :], PS[3][:, :], [psk(3)], [x["kv"]])

        hg_state = {"sidx": 0}
        KDT, AM, SQO = SB[I_KDT], SB[I_AM], SB[I_SQO]

        def hgB_T(x):
            if x["g"] == 0:
                C.op("dve", lambda e: e.memset(ZB[:, 0, :], 0.0), writes=[("ZB", 0)])
            kd = x["kt"]
            for j in range(4):
                C.op("pe", lambda e, j=j: e.transpose(out=PST[:, j * 128:(j + 1) * 128], in_=kd[:, j * 128:(j + 1) * 128],
                                                     identity=IDENT), reads=[x["kk"], "CB"], writes=[psk(7)])
            acopy(KDT[:, :], PST[:, 0:512], [psk(7)], [sbk(I_KDT)])

        def hgB_U(x):
            vr = x["vr"]
            for c in range(8):
                j, hf = divmod(c, 2)
                pr = slice(hf * 64, (hf + 1) * 64)
                ub = 4 + hf
                mm(PS[ub][:, j * 128:(j + 1) * 128], KDT[pr, j * 128:(j + 1) * 128], vr[pr, j * 128:(j + 1) * 128],
                   True, True, [sbk(I_KDT), x["kv"]], [psk(ub)])

        def hgB_A(x):
            qt, kt = x["qt"], x["kt"]
            for j in range(4):
                mm(PS[6][:, j * 128:(j + 1) * 128], kt[:, j * 128:(j + 1) * 128], qt[:, j * 128:(j + 1) * 128],
                   True, True, [x["kk"], x["kq"]], [psk(6)])
            vtt(AM[:, :].rearrange("p (a b) -> p a b", a=4), PS[6][:, :].rearrange("p (a b) -> p a b", a=4),
                MBLK.unsqueeze(1).broadcast_to([128, 4, 128]), ALU.mult, [psk(6), "CB"], [sbk(I_AM)])

        def hgB_REC(x):
            bi, par = x["bi"], x["par"]
            for c in range(8):
                ub = 4 + c % 2
                vstt(ZB[:, c + 1, :], ZB[:, c, :], 1.0 if c == 0 else EBL[:, bi, c - 1:c],
                     PS[ub][:, (c // 2) * 128:(c // 2 + 1) * 128], ALU.mult, ALU.add,
                     [("ZB", c), ("EBL", bi), psk(ub)], [("ZB", c + 1)])
            for hf in range(2):
                cs = range(4 * hf, 4 * hf + 4)
                vtt(SGB[:, par, 4 * hf:4 * hf + 4, :], ZB[:, 1 + 4 * hf:5 + 4 * hf, :],
                    EBL[:, bi, 4 * hf:4 * hf + 4].unsqueeze(2).broadcast_to([128, 4, 128]), ALU.mult,
                    [("ZB", c + 1) for c in cs] + [("EBL", bi)], [("SGB", par, c) for c in cs])
            vts(ZB[:, 0, :], ZB[:, 8, :], EBL[:, bi, 7:8], None, ALU.mult, None,
                [("ZB", 8), ("EBL", bi)], [("ZB", 0)])

        def hgB_O(x):
            g, par, qt, vr = x["g"], x["par"], x["qt"], x["vr"]
            for j in range(4):
                inter = []
                for hf in range(2):
                    c = 2 * j + hf
                    if c == 0:
                        if g == 0:
                            continue
                        inter.append((c, SGB[:, 1 - par, 7, :], ("SGB", 1 - par, 7)))
                    else:
                        inter.append((c, SGB[:, par, c - 1, :], ("SGB", par, c - 1)))
                mm(PS[6][:, j * 128:(j + 1) * 128], vr[:, j * 128:(j + 1) * 128], AM[:, j * 128:(j + 1) * 128],
                   True, len(inter) == 0, [x["kv"], sbk(I_AM)], [psk(6)])
                for ii, (c, st, stk) in enumerate(inter):
                    mm(PS[6][:, c * 64:(c + 1) * 64], st, qt[:, c * 64:(c + 1) * 64], False, ii == len(inter) - 1,
                       [stk, x["kq"]], [psk(6)])

        def hgB_N(x):
            hd, g, bi, t0 = x["hd"], x["g"], x["bi"], x["t0"]
            act(SQO[:, :], PS[6][:, :], AF.Square, [psk(6)], [sbk(I_SQO)])
            mm(PS[3][:, :], ONES, SQO[:, :], True, True, [sbk(I_SQO), "CB"], [psk(3)])

        def hgB_Nb(x):
            hd, g, bi, t0 = x["hd"], x["g"], x["bi"], x["t0"]
            r = rsqrt_from(PS[3][:, :], psk(3), 7, G, 1.0 / 128)
            vtt(SF[7][:, :], PS[6][:, :], r, ALU.mult, [psk(6), sfk(7)], [sfk(7)])
            vstt(MIX[:, 4 + hd, t0:t0 + G], SF[7][:, :], PRM[:, 44 + hd:45 + hd], SGTb[bi][:, :], ALU.mult, ALU.mult,
                 [sfk(7), sfk(5 + bi), "PRM"], [mixkey(4 + hd, g)])

        C.op("dve", lambda e: e.memset(SF[8][:, :], 0.0), writes=[sfk(8)])
        its = [(hd, g) for hd in range(4) for g in range(NGRP)]
        X_ = [hg_ctx(hd, g, i % 2) for i, (hd, g) in enumerate(its)]
        NI = len(its)
        hgA_F(X_[0]); hgA_Fd(X_[0]); hgA_Q(X_[0]); hgA_G(X_[0]); hgA_V(X_[0])
        hgA_F(X_[1]); hgA_Fd(X_[1]); hgA_Q(X_[1])
        hgB_T(X_[0]); hgB_U(X_[0])
        for i in range(NI):
            hgB_A(X_[i])
            if i + 1 < NI:
                hgA_V(X_[i + 1])
            if i + 2 < NI:
                hgA_F(X_[i + 2])
            hgB_REC(X_[i])
            hgB_O(X_[i])
            if i + 1 < NI:
                hgB_T(X_[i + 1])
                hgB_U(X_[i + 1])
            hgB_N(X_[i])
            if i + 2 < NI:
                hgA_Fd(X_[i + 2], "dve")
            hgB_Nb(X_[i])
            if i + 2 < NI:
                hgA_Fd(X_[i + 2], "act")
            if i + 1 < NI:
                hgA_G(X_[i + 1])
                if X_[i + 1]["g"] == NGRP - 1:
                    w_release(U_HG + 4 * X_[i + 1]["hd"] + 4)
            if i + 2 < NI:
                hgA_Q(X_[i + 2])
        tap("mix", MIX[:, :, :], [128, 8, S], [mixkey(c, g) for c in range(8) for g in range(NGRP)], BF16)
        stage("mix")

        def proj_add(u, m, src, srckeyf, nk, pbase):
            sl, sk = w_slot(u)
            for g in range(NGRP):
                pb = pbase + (g % 2)
                for c in range(nk):
                    mm(PS[pb][:, :], sl[:, c * 128:(c + 1) * 128], src[:, c, g * G:(g + 1) * G], c == 0, c == nk - 1,
                       [sk, srckeyf(c, g)], [psk(pb)])
                vtt(X[:, m, g * G:(g + 1) * G], PS[pb][:, :], X[:, m, g * G:(g + 1) * G], ALU.add,
                    [psk(pb), xkey(m, g)], [xkey(m, g)])

        for m in range(8):
            proj_add(U_WO + m, m, MIX, mixkey, 8, 2 * (m % 2))
            w_release(U_WO + m + 1)
        tap("x1", X[:, :, :], [128, 8, S], [xkey(c, g) for c in range(8) for g in range(NGRP)])
        stage("x1")

        for g in range(NGRP):
            norm_T(X, xkey, 8, HT, hkey, g * G, G, g)
        XO = BIG
        xokey = lambda c, g: ("XO", c, g)
        SGBF = SGB[:, :, :, :].rearrange("p a b c -> p (a b c)")

        def xq_ap(par, dc, t0, n):
            if par == 0:
                return XQ[:, dc, t0:t0 + n]
            if dc == 0:
                return SWAX[:, 2 * S + t0:2 * S + t0 + n]
            return SGBF[:, t0:t0 + n]

        def xqkey(par, dc, g):
            return ("XQ", par, dc, g)

        C.alias([xqkey(0, dc, g) for dc in range(2) for g in range(NGRP)],
                [("QA", 0, i) for i in range(4)] + [("KA", g) for g in range(NGRP)])
        C.alias([xqkey(1, 0, g) for g in range(NGRP)], [("VA", g) for g in range(NGRP)])
        C.alias([xqkey(1, 1, g) for g in range(NGRP)], [("SGB", pp, c) for pp in range(2) for c in range(8)])
        C.alias([xokey(c, g) for c in range(8) for g in range(NGRP)], [mixkey(c, g) for c in range(8) for g in range(NGRP)])

        def xq_piece(h, dc, g):
            u = U_XQ + 2 * h + dc
            pb = g % 2
            proj_T(u, pb, g)
            acopy(xq_ap(h % 2, dc, g * G, G), PS[pb][:, :], [psk(pb)], [xqkey(h % 2, dc, g)])
            if g == NGRP - 1:
                w_release(u + 1)

        def xq_pieces(h):
            return [(lambda dc=dc, g=g: xq_piece(h, dc, g)) for dc in range(2) for g in range(NGRP)]

        for f in xq_pieces(0):
            f()
        for h in range(4):
            par = h % 2
            pend = xq_pieces(h + 1) if h + 1 < 4 else []
            for g in range(NGRP):
                t0 = g * G
                for kt in range(2):
                    pb = 2 + kt
                    for dc in range(2):
                        mm(PS[pb][:, :], XKT[:, 2 * h + dc, kt * 128:(kt + 1) * 128], xq_ap(par, dc, t0, G), dc == 0, dc == 1,
                           [("XKT", 2 * h + dc), xqkey(par, dc, g)], [psk(pb)])
                    act(SB[kt][:, :], PS[pb][:, :], AF.Exp, [psk(pb)], [sbk(kt)], scale=1.0 / 16)
                if pend:
                    pend.pop(0)()
                for kt in range(2):
                    mm(PS[4][:, :], ONES, SB[kt][:, :], kt == 0, kt == 1, ["CB", sbk(kt)], [psk(4)])
                act(SF[0][:, :], PS[4][:, :], AF.Ln, [psk(4)], [sfk(0)])
                act(SF[0][:, :], SF[0][:, :], AF.Exp, [sfk(0)], [sfk(0)], scale=-1.0)
                if pend:
                    pend.pop(0)()
                for dc in range(2):
                    pb = 5 + dc
                    for kt in range(2):
                        mm(PS[pb][:, :], XV[:, kt, h * 256 + dc * 128:h * 256 + (dc + 1) * 128], SB[kt][:, :], kt == 0, kt == 1,
                           xvkeys + [sbk(kt)], [psk(pb)])
                    vtt(XO[:, 2 * h + dc, t0:t0 + G], PS[pb][:, :], SF[0][:, :], ALU.mult, [psk(pb), sfk(0)], [xokey(2 * h + dc, g)])
            assert not pend
        for m in range(8):
            proj_add(U_XO + m, m, XO, xokey, 8, 2 * (m % 2))
            w_release(U_XO + m + 1)
        tap("x2", X[:, :, :], [128, 8, S], [xkey(c, g) for c in range(8) for g in range(NGRP)])
        stage("x2")

        def final_norm(g):
            t0 = g * G
            ss = PS[6][:, :]
            for c in range(8):
                sq = SB[6 + (c % 2)][:, :]
                act(sq, X[:, c, t0:t0 + G], AF.Square, [xkey(c, g)], [sbk(6 + (c % 2))])
                mm(ss, ONES, sq, c == 0, c == 7, [sbk(6 + (c % 2)), "CB"], [psk(6)])
            r = rsqrt_from(ss, psk(6), 7, G, 1.0 / DM)
            for c in range(8):
                vstt(X[:, c, t0:t0 + G], X[:, c, t0:t0 + G], PRM[:, 32 + c:33 + c], r, ALU.mult, ALU.mult,
                     [xkey(c, g), sfk(7), "PRM"], [xkey(c, g)])
                if c % 4 == 3:
                    C.dma("sp", "out", yv[:, c - 3:c + 1, t0:t0 + G], X[:, c - 3:c + 1, t0:t0 + G],
                          reads=[xkey(cc, g) for cc in range(c - 3, c + 1)])

        for g in range(NGRP):
            norm_T(X, xkey, 24, HT, hkey, g * G, G, g)
        HID = BIG
        hidkey = lambda c, g: ("HID", c, g)
        C.alias([hidkey(c, g) for c in range(8) for g in range(NGRP)], [xokey(c, g) for c in range(8) for g in range(NGRP)])
        u = U_FFN
        for t, J in enumerate(THIRDS):
            for jj, j in enumerate(J):
                ugate, uup = u, u + 1
                for g in range(NGRP):
                    pa = 2 * (g % 2)
                    proj_T(ugate, pa, g)
                    proj_T(uup, pa + 1, g)
                    act(SF[g % 2][:, :], PS[pa][:, :], AF.Silu, [psk(pa)], [sfk(g % 2)])
                    vtt(HID[:, jj, g * G:(g + 1) * G], PS[pa + 1][:, :], SF[g % 2][:, :], ALU.mult,
                        [psk(pa + 1), sfk(g % 2)], [hidkey(jj, g)])
                u += 2
                w_release(u)
            if t < len(THIRDS) - 1:
                for m in range(8):
                    proj_add(u, m, HID, hidkey, len(J), 4)
                    u += 1
                    w_release(u)
            else:
                for g in range(NGRP):
                    for m in range(8):
                        sl, sk = w_slot(u + m)
                        pb = 4 + (m % 2)
                        for c in range(len(J)):
                            mm(PS[pb][:, :], sl[:, c * 128:(c + 1) * 128], HID[:, c, g * G:(g + 1) * G], c == 0, c == len(J) - 1,
                               [sk, hidkey(c, g)], [psk(pb)])
                        vtt(X[:, m, g * G:(g + 1) * G], PS[pb][:, :], X[:, m, g * G:(g + 1) * G], ALU.add,
                            [psk(pb), xkey(m, g)], [xkey(m, g)])
                    if g >= 1:
                        final_norm(g - 1)
                final_norm(NGRP - 1)
                u += 8
                w_release(u)
        assert u == NU

    except _Stop:
        pass
    for nm in ("out", "dbg"):
        if nm in C.dsem:
            d = C.dsem[nm]
            nc.sync.wait_ge(d[0], d[1])
    return nc, dbg_out


def _stat_unit(Wc):
    nk = Wc.shape[0] // 128
    out = np.zeros((128, 1024), np.float32)
    out[:, :nk * 128] = Wc.reshape(nk, 128, 128).transpose(1, 0, 2).reshape(128, nk * 128)
    return out


def _build_units(inp):
    w_in = np.asarray(inp["w_in"], np.float32)[0]
    w_out = np.asarray(inp["w_out"], np.float32)[0]
    w_xq = np.asarray(inp["w_xq"], np.float32)[0]
    w_xkv = np.asarray(inp["w_xkv"], np.float32)[0]
    w_xo = np.asarray(inp["w_xo"], np.float32)[0]
    w_gu = np.asarray(inp["w_gate_up"], np.float32)[0]
    w_dn = np.asarray(inp["w_down"], np.float32)[0]
    wu = np.zeros((NU, 128, 1024), np.float32)
    for m in range(8):
        wu[U_XK + m] = _stat_unit(w_xkv[:, m * 128:(m + 1) * 128])
    for hf in range(2):
        for j in range(4):
            blk = w_xkv[j * 256:(j + 1) * 256, 1024 + hf * 512:1024 + (hf + 1) * 512]
            wu[U_XV + hf * 4 + j] = blk.reshape(2, 128, 512).transpose(1, 0, 2).reshape(128, 1024)
    for hh in range(4):
        cols = np.concatenate([np.arange(hh * 64, hh * 64 + 64), np.arange((4 + hh) * 64, (4 + hh) * 64 + 64)])
        wu[U_QA + hh] = _stat_unit(w_in[:, cols])
    wu[U_KA] = _stat_unit(w_in[:, 512:640])
    wu[U_VA] = _stat_unit(w_in[:, 640:768])
    for hd in range(4):
        wu[U_HG + 4 * hd + 0] = _stat_unit(w_in[:, 1280 + hd * 128:1280 + (hd + 1) * 128])
        wu[U_HG + 4 * hd + 1] = _stat_unit(w_in[:, 768 + hd * 128:768 + (hd + 1) * 128])
        wu[U_HG + 4 * hd + 2] = _stat_unit(w_in[:, 2304 + hd * 128:2304 + (hd + 1) * 128])
        wu[U_HG + 4 * hd + 3] = _stat_unit(w_in[:, 1792 + hd * 128:1792 + (hd + 1) * 128])
    rows = []
    for c in range(4):
        rows.append(np.concatenate([np.arange(c * 64, c * 64 + 64), np.arange((4 + c) * 64, (4 + c) * 64 + 64)]))
    for c in range(4):
        rows.append(512 + c * 128 + np.arange(128))
    rows = np.concatenate(rows)
    w_out_p = w_out[rows, :]
    for m in range(8):
        wu[U_WO + m] = _stat_unit(w_out_p[:, m * 128:(m + 1) * 128])
        wu[U_XQ + m] = _stat_unit(w_xq[:, m * 128:(m + 1) * 128])
        wu[U_XO + m] = _stat_unit(w_xo[:, m * 128:(m + 1) * 128])
    for i, (kind, t, idx) in enumerate(FFN_UNITS):
        u = U_FFN + i
        if kind == "gate":
            wu[u] = _stat_unit(w_gu[:, idx * 128:(idx + 1) * 128])
        elif kind == "up":
            wu[u] = _stat_unit(w_gu[:, 2816 + idx * 128:2816 + (idx + 1) * 128])
        else:
            J = THIRDS[t]
            wu[u] = _stat_unit(w_dn[J[0] * 128:(J[-1] + 1) * 128, idx * 128:(idx + 1) * 128])
    return wu


def _build_prm(inp):
    prm = np.zeros((128, NPRM), np.float32)

    def cols(v):
        return np.asarray(v, np.float32).reshape(-1, 128).T

    prm[:, 0:8] = cols(inp["norm_mix"][0])
    prm[:, 8:16] = cols(inp["norm_xattn"][0])
    prm[:, 16:24] = cols(inp["norm_mem"][0])
    prm[:, 24:32] = cols(inp["norm_ffn"][0])
    prm[:, 32:40] = cols(inp["norm_final"])
    ag = np.asarray(inp["att_out_gain"], np.float32)[0]
    for hh in range(4):
        prm[0:64, 40 + hh] = ag[hh * 64:(hh + 1) * 64]
        prm[64:128, 40 + hh] = ag[(4 + hh) * 64:(5 + hh) * 64]
    prm[:, 44:48] = cols(inp["hg_out_gain"][0])
    lbl = np.asarray(inp["hg_lb_logits"], np.float32)
    prm[:, 48:52] = cols(lbl[0])
    prm[:, 52:56] = cols(lbl[1])
    sinks = np.asarray(inp["att_sinks"], np.float32)[0]
    for hh in range(4):
        prm[0:64, 58 + hh] = sinks[hh]
        prm[64:128, 58 + hh] = sinks[4 + hh]
    half = 8
    inv_freq = np.power(np.float32(500000.0), -np.arange(half, dtype=np.float32) * np.float32(2.0 / 16)).astype(np.float32)
    two_pi = 6.283185
    for p in range(128):
        d = p % 64
        if d < 8:
            prm[p, 56] = inv_freq[d] / np.float32(2 * np.pi)
            prm[p, 57] = -two_pi
        elif d < 16:
            prm[p, 56] = inv_freq[d - 8] / np.float32(2 * np.pi)
            prm[p, 57] = two_pi
    return prm


def _build_cst():
    cst = np.zeros((128, 6, 128), np.float32)
    k = np.arange(128)[:, None]
    q = np.arange(128)[None, :]
    cst[:, 0, :] = (k == q)
    cst[:, 1, :] = 1.0
    dq = q % 64
    cst[:, 2, :] = ((dq < 8) & (k == q + 8)) | ((dq >= 8) & (dq < 16) & (k == q - 8))
    cst[:, 3, :] = (k <= q)
    cst[:, 4, :] = (k > q)
    cst[:, 5, :] = ((k // 64) == (q // 64)) & (k <= q)
    return cst.reshape(128, 768)


_CACHE = {}


def _run(inputs, dbg=None, stop=None):
    key = (tuple(dbg or []), stop)
    if key not in _CACHE:
        _CACHE[key] = build(dbg, stop)
    nc, dbg_out = _CACHE[key]
    x = np.asarray(inputs["x"], np.float32)
    mem = np.asarray(inputs["mem"], np.float32)
    positions = np.asarray(inputs["positions"], np.int32)
    wu = _build_units(inputs)
    prm = _build_prm(inputs)
    cst = _build_cst()
    in_maps = []
    for b in range(8):
        in_maps.append({
            "xT": np.ascontiguousarray(x[b].T),
            "memT": np.ascontiguousarray(mem[b].T),
            "pos": np.ascontiguousarray(np.broadcast_to(positions[b][None, :], (128, S))),
            "wu": wu, "prm": prm, "cst": cst,
        })
    res = run_bass_kernel_spmd(nc, in_maps, core_ids=list(range(8)))
    return res


def kernel(**inputs):
    res = _run(inputs)
    y = np.stack([np.ascontiguousarray(r["yT"].T) for r in res.results], axis=0)
    return y.astype(np.float32)
```

```python
import numpy as np
import concourse.bass as bass
import concourse.mybir as mybir
from concourse.bass_utils import run_bass_kernel_spmd

F32 = mybir.dt.float32
BF16 = mybir.dt.bfloat16
I32 = mybir.dt.int32
AF = mybir.ActivationFunctionType
ALU = mybir.AluOpType

S = 2048
DM = 1024
G = 512
NGRP = 4
MEM = 256
NS = 9
EPS = 1e-6
NPRM = 64
THIRDS = [list(range(0, 8)), list(range(8, 15)), list(range(15, 22))]

U_XK = 0
U_XV = 8
U_QA = 16
U_KA = 20
U_VA = 21
U_HG = 22
U_WO = 38
U_XQ = 46
U_XO = 54
U_FFN = 62


def _ffn_units():
    lst = []
    for t, J in enumerate(THIRDS):
        for j in J:
            lst.append(("gate", t, j))
            lst.append(("up", t, j))
        for m in range(8):
            lst.append(("down", t, m))
    return lst


FFN_UNITS = _ffn_units()
NU = U_FFN + len(FFN_UNITS)


def _unit_cols(u):
    if u >= U_FFN:
        kind, t, _ = FFN_UNITS[u - U_FFN]
        if kind == "down":
            return len(THIRDS[t]) * 128
    return 1024


class _Eng:
    def __init__(self, name, eng, sem):
        self.name = name
        self.eng = eng
        self.sem = sem
        self.count = 0
        self.seen = {}
        self.pending = False


class Ctx:
    def __init__(self, nc):
        self.nc = nc
        self.E = {}
        for name, e in (("pe", nc.tensor), ("act", nc.scalar), ("dve", nc.vector),
                        ("pool", nc.gpsimd), ("sp", nc.sync)):
            self.E[name] = _Eng(name, e, nc.alloc_semaphore("s_" + name))
        self.keys = {}
        self.dsem = {}

    def _deps(self, reads, writes):
        need = {}

        def add(tok):
            sid, sem, val = tok
            if sid not in need or need[sid][1] < val:
                need[sid] = (sem, val)

        for k in reads:
            st = self.keys.get(k)
            if st is not None and st[0] is not None:
                add(st[0])
        for k in writes:
            st = self.keys.get(k)
            if st is not None:
                if st[0] is not None:
                    add(st[0])
                for t in st[1].values():
                    add(t)
        return need

    def _wait(self, es, need):
        for sid, (sem, val) in need.items():
            if sid.startswith("d_"):
                val = self.dsem[sid[2:]][1]
            if es.seen.get(sid, 0) >= val:
                continue
            if sid == es.name and es.name == "pe":
                continue
            if sid == es.name and es.pending and val > es.count:
                raise RuntimeError("wait on own pending instruction")
            es.eng.wait_ge(sem, val)
            es.seen[sid] = val

    def _record(self, tok, reads, writes):
        for k in reads:
            st = self.keys.setdefault(k, [None, {}])
            old = st[1].get(tok[0])
            if old is None or old[2] < tok[2]:
                st[1][tok[0]] = tok
        for k in writes:
            self.keys[k] = [tok, {}]

    def op(self, engname, fn, reads=(), writes=(), inc=True):
        inc = True
        psr = [k for k in reads if isinstance(k, tuple) and k[0] == "ps"]
        if psr:
            reads = [k for k in reads if k not in psr]
            writes = list(writes) + psr
        es = self.E[engname]
        self._wait(es, self._deps(reads, writes))
        ins = fn(es.eng)
        if inc:
            es.count += 1
            ins.then_inc(es.sem, 1)
            tok = (es.name, es.sem, es.count)
            es.pending = False
        else:
            tok = (es.name, es.sem, es.count + 1)
            es.pending = True
        self._record(tok, reads, writes)
        return tok

    def dma(self, queue, semname, out, in_, reads=(), writes=()):
        es = self.E[queue]
        self._wait(es, self._deps(reads, writes))
        d = self.dsem.get(semname)
        if d is None:
            d = [self.nc.alloc_semaphore("d_" + semname), 0]
            self.dsem[semname] = d
        d[1] += 16
        es.eng.dma_start(out=out, in_=in_).then_inc(d[0], 16)
        tok = ("d_" + semname, d[0], d[1])
        self._record(tok, reads, writes)
        return tok

    def alias(self, newkeys, oldkeys):
        toks = {}
        for k in oldkeys:
            st = self.keys.get(k)
            if st is None:
                continue
            cand = list(st[1].values())
            if st[0] is not None:
                cand.append(st[0])
            for t in cand:
                if t[0] not in toks or toks[t[0]][2] < t[2]:
                    toks[t[0]] = t
        for k in newkeys:
            st = self.keys.setdefault(k, [None, {}])
            for sid, t in toks.items():
                if sid not in st[1] or st[1][sid][2] < t[2]:
                    st[1][sid] = t


class _Stop(Exception):
    pass


def build(dbg=None, stop=None):
    nc = bass.Bass("TRN2", target_bir_lowering=False)
    C = Ctx(nc)
    dbg = dbg or []
    dbg_out = {}

    xT = nc.dram_tensor("xT", [DM, S], F32, kind="ExternalInput").ap()
    memT = nc.dram_tensor("memT", [DM, MEM], F32, kind="ExternalInput").ap()
    pos = nc.dram_tensor("pos", [128, S], I32, kind="ExternalInput").ap()
    wu = nc.dram_tensor("wu", [NU, 128, 1024], F32, kind="ExternalInput").ap()
    prm = nc.dram_tensor("prm", [128, NPRM], F32, kind="ExternalInput").ap()
    cst = nc.dram_tensor("cst", [128, 6 * 128], F32, kind="ExternalInput").ap()
    yT = nc.dram_tensor("yT", [DM, S], F32, kind="ExternalOutput").ap()
    yv = yT.rearrange("(c p) t -> p c t", p=128)

    X = nc.alloc_sbuf_tensor("X", [128, 8, S], F32)
    HT = nc.alloc_sbuf_tensor("HT", [128, 8, S], BF16)
    BIG = nc.alloc_sbuf_tensor("BIG", [128, 8, S], BF16)
    RING = nc.alloc_sbuf_tensor("RING", [128, NS, 1024], BF16)
    XKT = nc.alloc_sbuf_tensor("XKT", [128, 8, MEM], BF16)
    XV = nc.alloc_sbuf_tensor("XV", [128, 2, DM], BF16)
    PRM = nc.alloc_sbuf_tensor("PRM", [128, NPRM], F32)
    CB = nc.alloc_sbuf_tensor("CB", [128, 6, 128], BF16)
    LBT = nc.alloc_sbuf_tensor("LBT", [128, 20], F32)
    NSF = 10
    NSB = 11
    SF = [nc.alloc_sbuf_tensor("SF%d" % i, [128, G], F32) for i in range(NSF)]
    SB = [nc.alloc_sbuf_tensor("SB%d" % i, [128, G], BF16) for i in range(NSB)]
    SWAX = nc.alloc_sbuf_tensor("SWAX", [128, 3 * S], BF16)
    QA = SWAX[:, 0:S].rearrange("p (a b) -> p a b", a=4)
    KA = SWAX[:, S:2 * S]
    VA = SWAX[:, 2 * S:3 * S].rearrange("p (a b) -> p a b", a=16)
    XQ = SWAX[:, 0:2 * S].rearrange("p (a b) -> p a b", a=2)
    SGB = nc.alloc_sbuf_tensor("SGB", [128, 2, 8, 128], BF16)
    ZB = nc.alloc_sbuf_tensor("ZB", [128, 9, 128], F32)
    EBL = nc.alloc_sbuf_tensor("EBL", [128, 2, 8], F32)
    ES = nc.alloc_sbuf_tensor("ES", [128, 4], F32)
    LN2C = nc.alloc_sbuf_tensor("LN2C", [128, 1], F32)
    PS = [nc.alloc_psum_tensor("PS%d" % i, [128, G], F32) for i in range(7)]
    PST = nc.alloc_psum_tensor("PST", [128, 2 * G], BF16)

    IDENT = CB[:, 0, :]
    ONES = CB[:, 1, :]
    PSW = CB[:, 2, :]
    MCUR = CB[:, 3, :]
    MPREV = CB[:, 4, :]
    MBLK = CB[:, 5, :]

    def psk(i):
        return ("ps", i)

    def sfk(i):
        return ("sf", i)

    def sbk(i):
        return ("sb", i)

    class WS:
        issued = 0
        rel = 0

    def w_pump():
        while WS.issued < min(NU, WS.rel + NS):
            u = WS.issued
            sl = u % NS
            ncol = _unit_cols(u)
            C.dma("pool", "ring%d" % sl, RING[:, sl, 0:ncol], wu[u, :, 0:ncol], writes=[("ring", sl)])
            WS.issued += 1

    def w_slot(u):
        assert u < WS.issued and u >= WS.rel, (u, WS.issued, WS.rel)
        return RING[:, u % NS, :], ("ring", u % NS)

    def w_release(upto):
        if upto > WS.rel:
            WS.rel = upto
        w_pump()

    def mm(out, lhsT, rhs, start, stop, reads, writes, inc=None):
        if inc is None:
            inc = stop
        return C.op("pe", lambda e: e.matmul(out, lhsT=lhsT, rhs=rhs, start=start, stop=stop),
                    reads=reads, writes=writes, inc=inc)

    def act(out, in_, func, reads, writes, scale=1.0, bias=0.0):
        return C.op("act", lambda e: e.activation(out=out, in_=in_, func=func, scale=scale, bias=bias),
                    reads=reads, writes=writes)

    def acopy(out, in_, reads, writes):
        return C.op("act", lambda e: e.copy(out=out, in_=in_), reads=reads, writes=writes)

    def vtt(out, in0, in1, op, reads, writes, eng="dve"):
        return C.op(eng, lambda e: e.tensor_tensor(out=out, in0=in0, in1=in1, op=op), reads=reads, writes=writes)

    def vts(out, in0, s1, s2, op0, op1, reads, writes, eng="dve"):
        if s2 is None:
            return C.op(eng, lambda e: e.tensor_scalar(out=out, in0=in0, scalar1=s1, scalar2=None, op0=op0),
                        reads=reads, writes=writes)
        return C.op(eng, lambda e: e.tensor_scalar(out=out, in0=in0, scalar1=s1, scalar2=s2, op0=op0, op1=op1),
                    reads=reads, writes=writes)

    def vstt(out, in0, scalar, in1, op0, op1, reads, writes):
        return C.op("dve", lambda e: e.scalar_tensor_tensor(out=out, in0=in0, scalar=scalar, in1=in1, op0=op0, op1=op1),
                    reads=reads, writes=writes)

    def vcopy(out, in_, reads, writes, eng="dve"):
        return C.op(eng, lambda e: e.tensor_copy(out=out, in_=in_), reads=reads, writes=writes)

    def vrecip(out, in_, reads, writes):
        return C.op("dve", lambda e: e.reciprocal(out=out, in_=in_), reads=reads, writes=writes)

    def rsqrt_from(ps_ap, pskey, sf_i, ncol, inv_n):
        r = SF[sf_i][:, 0:ncol]
        act(r, ps_ap, AF.Ln, [pskey], [sfk(sf_i)], scale=inv_n, bias=EPS)
        act(r, r, AF.Exp, [sfk(sf_i)], [sfk(sf_i)], scale=-0.5)
        return r

    def tap(name, ap, shape, keys, dtype=F32):
        if name not in dbg:
            return
        d = nc.dram_tensor("dbg_" + name, list(shape), F32, kind="ExternalOutput").ap()
        dbg_out[name] = d
        C.dma("pool" if dtype != F32 else "sp", "dbg", d, ap, reads=keys)

    def stage(name):
        if stop == name:
            raise _Stop()

    try:
        C.dma("sp", "prm", PRM[:, :], prm[:, :], writes=["PRM"])
        C.dma("pool", "cst", CB[:, :, :].rearrange("p a b -> p (a b)"), cst[:, :], writes=["CB"])
        w_pump()
        MEMX = BIG[:, 0:2, :].rearrange("p a b -> p (a b)").bitcast(F32).rearrange("p (c t) -> p c t", c=8)
        MN = BIG[:, 2, 0:2048].rearrange("p (c t) -> p c t", c=8)
        for c in range(8):
            C.dma("sp", "mem", MEMX[:, c, :], memT[c * 128:(c + 1) * 128, :], writes=[("MEMX", c)])
        xv = xT.rearrange("(c p) t -> p c t", p=128)
        for g in range(NGRP):
            for h4 in range(2):
                C.dma("sp", "x%d" % g, X[:, 4 * h4:4 * h4 + 4, g * G:(g + 1) * G], xv[:, 4 * h4:4 * h4 + 4, g * G:(g + 1) * G],
                      writes=[("X", c, g) for c in range(4 * h4, 4 * h4 + 4)])

        vtt(LBT[:, 0:4], PRM[:, 48:52], PRM[:, 52:56], ALU.subtract, ["PRM"], ["LB0"])
        act(LBT[:, 0:4], LBT[:, 0:4], AF.Tanh, ["LB0"], ["LB0"], scale=0.5)
        vts(LBT[:, 4:8], LBT[:, 0:4], -0.25, 0.25, ALU.mult, ALU.add, ["LB0"], ["LB1"])
        vts(LBT[:, 8:12], LBT[:, 0:4], 0.25, 0.75, ALU.mult, ALU.add, ["LB0"], ["LB2"])
        vts(LBT[:, 12:16], LBT[:, 0:4], 0.25, -0.25, ALU.mult, ALU.add, ["LB0"], ["LB3"])
        vts(LBT[:, 16:20], LBT[:, 0:4], -0.25, 0.25, ALU.mult, ALU.add, ["LB0"], ["LB4"])
        act(ES[:, :], PRM[:, 58:62], AF.Exp, ["PRM"], ["ES"])
        C.op("dve", lambda e: e.memset(LN2C[:, :], 20.79441541679836), writes=["LN2C"])

        stage("setup")
        def norm_T(src, srckey, gcol, dst, dstkey, t0, ncol, gi):
            ss = PS[6][:, 0:ncol]
            for c in range(8):
                sq = SB[6 + (c % 2)][:, 0:ncol]
                act(sq, src[:, c, t0:t0 + ncol], AF.Square, [srckey(c, gi)], [sbk(6 + (c % 2))])
                mm(ss, ONES, sq, c == 0, c == 7, [sbk(6 + (c % 2)), "CB"], [psk(6)])
            r = rsqrt_from(ss, psk(6), 7, ncol, 1.0 / DM)
            for c in range(8):
                vstt(dst[:, c, t0:t0 + ncol], src[:, c, t0:t0 + ncol], PRM[:, gcol + c:gcol + c + 1], r,
                     ALU.mult, ALU.mult, [srckey(c, gi), sfk(7), "PRM"], [dstkey(c, gi)])

        xkey = lambda c, g: ("X", c, g)
        hkey = lambda c, g: ("HT", c, g)

        norm_T(MEMX, lambda c, g: ("MEMX", c), 16, MN, lambda c, g: ("MN", c), 0, MEM, 0)
        mnkeys = [("MN", c) for c in range(8)]
        tap("mn", MN, [128, 8, MEM], mnkeys, BF16)
        stage("mn")
        for m in range(8):
            sl, sk = w_slot(U_XK + m)
            pb = m % 2
            for c in range(8):
                mm(PS[pb][:, 0:MEM], sl[:, c * 128:(c + 1) * 128], MN[:, c, :], c == 0, c == 7,
                   [sk, ("MN", c)], [psk(pb)])
            acopy(XKT[:, m, :], PS[pb][:, 0:MEM], [psk(pb)], [("XKT", m)])
            w_release(U_XK + m + 1)
        for hf in range(2):
            for tt in range(2):
                pb = 2 + tt
                for c in range(8):
                    sl, sk = w_slot(U_XV + hf * 4 + c // 2)
                    mm(PS[pb][:, :], MN[:, c, tt * 128:(tt + 1) * 128], sl[:, (c % 2) * 512:(c % 2 + 1) * 512],
                       c == 0, c == 7, [sk, ("MN", c)], [psk(pb)])
                acopy(XV[:, tt, hf * 512:(hf + 1) * 512], PS[pb][:, :], [psk(pb)], [("XV", tt, hf)])
            w_release(U_XV + hf * 4 + 4)
        xvkeys = [("XV", tt, hf) for tt in range(2) for hf in range(2)]
        tap("xkt", XKT[:, :, :], [128, 8, MEM], [("XKT", m) for m in range(8)], BF16)
        tap("xv", XV[:, :, :], [128, 2, DM], xvkeys, BF16)
        stage("xv")

        for g in range(NGRP):
            norm_T(X, xkey, 0, HT, hkey, g * G, G, g)
        tap("ht", HT[:, :, :], [128, 8, S], [hkey(c, g) for c in range(8) for g in range(NGRP)], BF16)
        stage("ht")

        MIX = BIG
        mixkey = lambda c, g: ("MIX", c, g)
        C.alias([mixkey(c, g) for c in range(8) for g in range(NGRP)],
                [("MEMX", c) for c in range(8)] + mnkeys)

        def proj_T(u, pb, g):
            sl, sk = w_slot(u)
            for c in range(8):
                mm(PS[pb][:, :], sl[:, c * 128:(c + 1) * 128], HT[:, c, g * G:(g + 1) * G], c == 0, c == 7,
                   [sk, hkey(c, g)], [psk(pb)])

        TWO_PI = 6.283185

        def qakey(bi, i):
            return ("QA", bi, i)

        def swa_tables(g):
            t0 = g * G
            ci, si = 0, 1
            posi = SF[2][:, :].bitcast(I32)
            C.dma("sp", "pos", posi, pos[:, t0:t0 + G], writes=[sfk(2)])
            vcopy(SF[3][:, :], posi, [sfk(2)], [sfk(3)])
            vts(SF[3][:, :], SF[3][:, :], PRM[:, 56:57], None, ALU.mult, None, [sfk(3), "PRM"], [sfk(3)])
            ni = SF[2][:, :].bitcast(I32)
            vcopy(ni, SF[3][:, :], [sfk(3)], [sfk(2)])
            vcopy(SF[4][:, :], ni, [sfk(2)], [sfk(4)])
            vtt(SF[4][:, :], SF[3][:, :], SF[4][:, :], ALU.subtract, [sfk(3), sfk(4)], [sfk(4)])
            act(SF[si][:, :], SF[4][:, :], AF.Sin, [sfk(4), "PRM"], [sfk(si)], scale=PRM[:, 57:58])
            vts(SF[3][:, :], SF[3][:, :], 0.25, None, ALU.add, None, [sfk(3)], [sfk(3)])
            vcopy(ni, SF[3][:, :], [sfk(3)], [sfk(2)])
            vcopy(SF[4][:, :], ni, [sfk(2)], [sfk(4)])
            vtt(SF[4][:, :], SF[3][:, :], SF[4][:, :], ALU.subtract, [sfk(3), sfk(4)], [sfk(4)])
            act(SF[ci][:, :], SF[4][:, :], AF.Sin, [sfk(4)], [sfk(ci)], scale=TWO_PI)

        def swa_proj(g, i):
            t0 = g * G
            ci, si = 0, 1
            pb = 2 + 2 * (i % 2)
            proj_T(U_QA + i, pb, g)
            acopy(SB[7 + (i % 2)][:, :], PS[pb][:, :], [psk(pb)], [sbk(7 + (i % 2))])
            mm(PS[pb + 1][:, :], PSW, SB[7 + (i % 2)][:, :], True, True, [sbk(7 + (i % 2)), "CB"], [psk(pb + 1)])
            t1, t2 = SF[2 + 2 * (i % 2)], SF[3 + 2 * (i % 2)]
            k1, k2 = sfk(2 + 2 * (i % 2)), sfk(3 + 2 * (i % 2))
            vtt(t1[:, :], PS[pb][:, :], SF[ci][:, :], ALU.mult, [psk(pb), sfk(ci)], [k1])
            vtt(t2[:, :], PS[pb + 1][:, :], SF[si][:, :], ALU.mult, [psk(pb + 1), sfk(si)], [k2])
            if i < 4:
                vtt(QA[:, i, :], t1[:, :], t2[:, :], ALU.add, [k1, k2], [qakey(0, i)])
            else:
                vtt(KA[:, t0:t0 + G], t1[:, :], t2[:, :], ALU.add, [k1, k2], [("KA", g)])

        def swa_v(g):
            t0 = g * G
            sl, sk = w_slot(U_VA)
            for tt in range(4):
                for c in range(8):
                    mm(PS[6][:, tt * 128:(tt + 1) * 128], HT[:, c, t0 + tt * 128:t0 + (tt + 1) * 128],
                       sl[:, c * 128:(c + 1) * 128], c == 0, c == 7, [sk, hkey(c, g)], [psk(6)])
            acopy(VA[:, 4 * g:4 * g + 4, :].rearrange("p a b -> p (a b)"), PS[6][:, :], [psk(6)], [("VA", g)])

        PT_IDX = [[0, 1, 2, 3], [7, 8, 9, 10]]
        OD_BANK = [(2, 3), (4, 5)]
        TAIL_SF = [(7, 8), (5, 6)]

        def swa_front_parts(n):
            g, j = divmod(n, 4)
            st = n % 2
            q0 = j * 128
            kts = [1] if n == 0 else [0, 1]
            ob, db = OD_BANK[st]
            pt = {}
            combos = [(kt, gg) for kt in kts for gg in range(2)]

            def sc(idx):
                kt, gg = combos[idx]
                pr = slice(gg * 64, (gg + 1) * 64)
                kb = n - 1 + kt
                sb_i = idx % 2
                mm(PS[sb_i][:, :].rearrange("p (a b) -> p a b", a=4), KA[pr, kb * 128:(kb + 1) * 128],
                   QA[pr, :, q0:q0 + 128], True, True,
                   [("KA", kb // 4)] + [qakey(0, i) for i in range(4)], [psk(sb_i)])
                et = SB[4 + (idx % 2)]
                act(et[:, :], PS[sb_i][:, :], AF.Exp, [psk(sb_i)], [sbk(4 + (idx % 2))], scale=0.125)
                msk = MCUR if kt == 1 else MPREV
                pi = PT_IDX[st][idx]
                vtt(SB[pi][:, :].rearrange("p (a b) -> p a b", a=4), et[:, :].rearrange("p (a b) -> p a b", a=4),
                    msk.unsqueeze(1).broadcast_to([128, 4, 128]), ALU.mult,
                    [sbk(4 + (idx % 2)), "CB"], [sbk(pi)])
                pt[(gg, kt)] = pi

            def part1():
                for idx in range(min(2, len(combos))):
                    sc(idx)

            def part2():
                for idx in range(2, len(combos)):
                    sc(idx)

            def part3():
                for ii, kt in enumerate(kts):
                    kb = n - 1 + kt
                    for gg in range(2):
                        pr = slice(gg * 64, (gg + 1) * 64)
                        mm(PS[ob][pr, :], VA[:, kb, gg * 64:(gg + 1) * 64], SB[pt[(gg, kt)]][:, :], ii == 0, ii == len(kts) - 1,
                           [("VA", kb // 4), sbk(pt[(gg, kt)])], [psk(ob)])
                for ii, kt in enumerate(kts):
                    for gg in range(2):
                        pr = slice(gg * 64, (gg + 1) * 64)
                        mm(PS[db][pr, :], ONES[:, 0:64], SB[pt[(gg, kt)]][:, :], ii == 0, ii == len(kts) - 1,
                           ["CB", sbk(pt[(gg, kt)])], [psk(db)])

            return part1, part2, part3

        def swa_tail_parts(n):
            g, j = divmod(n, 4)
            st = n % 2
            t0 = g * G
            q0 = j * 128
            ob, db = OD_BANK[st]
            fd, fa = TAIL_SF[st]

            def part1():
                for hh in range(4):
                    act(SF[fd][:, hh * 128:(hh + 1) * 128], PS[db][:, hh * 128:(hh + 1) * 128], AF.Ln, [psk(db), "ES"], [sfk(fd)],
                        bias=ES[:, hh:hh + 1])
                act(SF[fd][:, :], SF[fd][:, :], AF.Exp, [sfk(fd)], [sfk(fd)], scale=-1.0)
                vtt(SF[fa][:, :], PS[ob][:, :], SF[fd][:, :], ALU.mult, [psk(ob), sfk(fd)], [sfk(fa)])

            def part2():
                act(SB[6][:, :], SF[fa][:, :], AF.Square, [sfk(fa)], [sbk(6)])
                for hh in range(4):
                    mm(PS[6][:, 0:128], ONES, SB[6][:, hh * 128:(hh + 1) * 128], hh == 0, hh == 3, [sbk(6), "CB"], [psk(6)])

            def part3():
                r = rsqrt_from(PS[6][:, 0:128], psk(6), 9, 128, 1.0 / 512)
                for hh in range(4):
                    vstt(MIX[:, hh, t0 + q0:t0 + q0 + 128], SF[fa][:, hh * 128:(hh + 1) * 128], PRM[:, 40 + hh:41 + hh], r,
                         ALU.mult, ALU.mult, [sfk(fa), sfk(9), "PRM"], [mixkey(hh, g)])

            return part1, part2, part3

        def swa_blocks(n0):
            F = [swa_front_parts(n0 + k) for k in range(4)]
            T = [swa_tail_parts(n0 + k) for k in range(4)]
            for k in range(2):
                for f in F[k]:
                    f()
            for k in range(4):
                nf = F[k + 2] if k + 2 < 4 else (lambda: None, lambda: None, lambda: None)
                T[k][0]()
                nf[0]()
                T[k][1]()
                nf[1]()
                T[k][2]()
                nf[2]()

        for g in range(NGRP):
            swa_tables(g)
            for i in range(5):
                swa_proj(g, i)
            swa_v(g)
            if g == 0:
                tap("qa0", QA[:, :, :], [128, 4, G], [qakey(0, i) for i in range(4)], BF16)
            n0 = 4 * g
            swa_blocks(n0)
        w_release(U_VA + 1)
        tap("mixa", MIX[:, 0:4, :], [128, 4, S], [mixkey(c, g) for c in range(4) for g in range(NGRP)], BF16)
        stage("mixa")

        QTb = [SB[0], SB[1]]
        KTb = [SB[2], SB[3]]
        KDb = [SB[4], SB[5]]
        VRb = [SB[6], SB[7]]
        I_KDT, I_AM, I_SQO = 8, 9, 10
        SGTb = [SF[5], SF[6]]

        def hg_ctx(hd, g, bi):
            d = dict(hd=hd, g=g, bi=bi, t0=g * G, par=g % 2,
                     qt=QTb[bi], kt=KTb[bi], kd=KDb[bi], vr=VRb[bi],
                     kq=sbk(bi), kk=sbk(2 + bi), kkd=sbk(4 + bi), kv=sbk(6 + bi))
            d["uf"], d["uq"], d["ug"], d["ui"] = (U_HG + 4 * hd + k for k in range(4))
            d["ik"] = 0 if bi == 0 else 9
            return d

        def hgA_F(x):
            hd, g, bi, ik = x["hd"], x["g"], x["bi"], x["ik"]
            proj_T(x["uf"], 0, g)
            act(SF[ik][:, :], PS[0][:, :], AF.Tanh, [psk(0)], [sfk(ik)], scale=0.5)
            act(SF[3][:, :], SF[ik][:, :], AF.Identity, [sfk(ik), "LB1", "LB2"], [sfk(3)],
                scale=LBT[:, 4 + hd:5 + hd], bias=LBT[:, 8 + hd:9 + hd])
            act(SF[ik][:, :], SF[ik][:, :], AF.Identity, [sfk(ik), "LB3", "LB4"], [sfk(ik)],
                scale=LBT[:, 12 + hd:13 + hd], bias=LBT[:, 16 + hd:17 + hd])

        def hgA_Fd(x, part="all"):
            bi = x["bi"]
            P3 = SF[1][:, :].rearrange("p (c s) -> p c s", s=64)
            if part in ("all", "dve"):
                F3 = SF[3][:, :].rearrange("p (c s) -> p c s", s=64)
                Z3 = SF[8][:, :].rearrange("p (c s) -> p c s", s=64)
                vcopy(Z3[:, :, 0], F3[:, :, 0], [sfk(3)], [sfk(8)])
                C.op("dve", lambda e: e.tensor_tensor_scan(out=SF[1][:, :], data0=SF[3][:, :], data1=SF[8][:, :], initial=1.0,
                                                           op0=ALU.mult, op1=ALU.max), reads=[sfk(3), sfk(8)], writes=[sfk(1)])
                vcopy(EBL[:, bi, :], P3[:, :, 63], [sfk(1)], [("EBL", bi)])
            if part in ("all", "act"):
                act(SF[2][:, :], SF[1][:, :], AF.Ln, [sfk(1)], [sfk(2)], scale=float(2 ** 30))
                act(SF[2][:, :], SF[2][:, :], AF.Exp, [sfk(2), "LN2C"], [sfk(2)], scale=-1.0, bias=LN2C[:, 0:1])

        def hgA_Q(x):
            g = x["g"]
            P3 = SF[1][:, :].rearrange("p (c s) -> p c s", s=64)
            proj_T(x["uq"], 1, g)
            act(SF[4][:, :], PS[1][:, :], AF.Silu, [psk(1)], [sfk(4)])
            vtt(x["qt"][:, :], SF[4][:, :], SF[1][:, :], ALU.mult, [sfk(4), sfk(1)], [x["kq"]], eng="pool")
            vtt(x["kt"][:, :], SF[x["ik"]][:, :], SF[2][:, :], ALU.mult, [sfk(x["ik"]), sfk(2)], [x["kk"]], eng="pool")

        def hgA_G(x):
            proj_T(x["ug"], 2, x["g"])
            act(SGTb[x["bi"]][:, :], PS[2][:, :], AF.Silu, [psk(2)], [sfk(5 + x["bi"])])

        def hgA_V(x):
            g, t0 = x["g"], x["t0"]
            sl, sk = w_slot(x["ui"])
            for tt in range(4):
                for c in range(8):
                    mm(PS[3][:, tt * 128:(tt + 1) * 128], HT[:, c, t0 + tt * 128:t0 + (tt + 1) * 128],
                       sl[:, c * 128:(c + 1) * 128], c == 0, c == 7, [sk, hkey(c, g)], [psk(3)])
            acopy(x["vr"][:, :], PS[3][:, :], [psk(3)], [x["kv"]])

        hg_state = {"sidx": 0}
        KDT, AM, SQO = SB[I_KDT], SB[I_AM], SB[I_SQO]

        def hgB_T(x):
            if x["g"] == 0:
                C.op("dve", lambda e: e.memset(ZB[:, 0, :], 0.0), writes=[("ZB", 0)])
            kd = x["kt"]
            for j in range(4):
                C.op("pe", lambda e, j=j: e.transpose(out=PST[:, j * 128:(j + 1) * 128], in_=kd[:, j * 128:(j + 1) * 128],
                                                     identity=IDENT), reads=[x["kk"], "CB"], writes=[psk(7)])
            acopy(KDT[:, :], PST[:, 0:512], [psk(7)], [sbk(I_KDT)])

        def hgB_U(x):
            vr = x["vr"]
            for c in range(8):
                j, hf = divmod(c, 2)
                pr = slice(hf * 64, (hf + 1) * 64)
                ub = 4 + hf
                mm(PS[ub][:, j * 128:(j + 1) * 128], KDT[pr, j * 128:(j + 1) * 128], vr[pr, j * 128:(j + 1) * 128],
                   True, True, [sbk(I_KDT), x["kv"]], [psk(ub)])

        def hgB_A(x):
            qt, kt = x["qt"], x["kt"]
            for j in range(4):
                mm(PS[6][:, j * 128:(j + 1) * 128], kt[:, j * 128:(j + 1) * 128], qt[:, j * 128:(j + 1) * 128],
                   True, True, [x["kk"], x["kq"]], [psk(6)])
            vtt(AM[:, :].rearrange("p (a b) -> p a b", a=4), PS[6][:, :].rearrange("p (a b) -> p a b", a=4),
                MBLK.unsqueeze(1).broadcast_to([128, 4, 128]), ALU.mult, [psk(6), "CB"], [sbk(I_AM)])

        def hgB_REC(x):
            bi, par = x["bi"], x["par"]
            for c in range(8):
                ub = 4 + c % 2
                vstt(ZB[:, c + 1, :], ZB[:, c, :], 1.0 if c == 0 else EBL[:, bi, c - 1:c],
                     PS[ub][:, (c // 2) * 128:(c // 2 + 1) * 128], ALU.mult, ALU.add,
                     [("ZB", c), ("EBL", bi), psk(ub)], [("ZB", c + 1)])
            for hf in range(2):
                cs = range(4 * hf, 4 * hf + 4)
                vtt(SGB[:, par, 4 * hf:4 * hf + 4, :], ZB[:, 1 + 4 * hf:5 + 4 * hf, :],
                    EBL[:, bi, 4 * hf:4 * hf + 4].unsqueeze(2).broadcast_to([128, 4, 128]), ALU.mult,
                    [("ZB", c + 1) for c in cs] + [("EBL", bi)], [("SGB", par, c) for c in cs])
            vts(ZB[:, 0, :], ZB[:, 8, :], EBL[:, bi, 7:8], None, ALU.mult, None,
                [("ZB", 8), ("EBL", bi)], [("ZB", 0)])

        def hgB_O(x):
            g, par, qt, vr = x["g"], x["par"], x["qt"], x["vr"]
            for j in range(4):
                inter = []
                for hf in range(2):
                    c = 2 * j + hf
                    if c == 0:
                        if g == 0:
                            continue
                        inter.append((c, SGB[:, 1 - par, 7, :], ("SGB", 1 - par, 7)))
                    else:
                        inter.append((c, SGB[:, par, c - 1, :], ("SGB", par, c - 1)))
                mm(PS[6][:, j * 128:(j + 1) * 128], vr[:, j * 128:(j + 1) * 128], AM[:, j * 128:(j + 1) * 128],
                   True, len(inter) == 0, [x["kv"], sbk(I_AM)], [psk(6)])
                for ii, (c, st, stk) in enumerate(inter):
                    mm(PS[6][:, c * 64:(c + 1) * 64], st, qt[:, c * 64:(c + 1) * 64], False, ii == len(inter) - 1,
                       [stk, x["kq"]], [psk(6)])

        def hgB_N(x):
            hd, g, bi, t0 = x["hd"], x["g"], x["bi"], x["t0"]
            act(SQO[:, :], PS[6][:, :], AF.Square, [psk(6)], [sbk(I_SQO)])
            mm(PS[3][:, :], ONES, SQO[:, :], True, True, [sbk(I_SQO), "CB"], [psk(3)])

        def hgB_Nb(x):
            hd, g, bi, t0 = x["hd"], x["g"], x["bi"], x["t0"]
            r = rsqrt_from(PS[3][:, :], psk(3), 7, G, 1.0 / 128)
            vtt(SF[7][:, :], PS[6][:, :], r, ALU.mult, [psk(6), sfk(7)], [sfk(7)])
            vstt(MIX[:, 4 + hd, t0:t0 + G], SF[7][:, :], PRM[:, 44 + hd:45 + hd], SGTb[bi][:, :], ALU.mult, ALU.mult,
                 [sfk(7), sfk(5 + bi), "PRM"], [mixkey(4 + hd, g)])

        C.op("dve", lambda e: e.memset(SF[8][:, :], 0.0), writes=[sfk(8)])
        its = [(hd, g) for hd in range(4) for g in range(NGRP)]
        X_ = [hg_ctx(hd, g, i % 2) for i, (hd, g) in enumerate(its)]
        NI = len(its)
        hgA_F(X_[0]); hgA_Fd(X_[0]); hgA_Q(X_[0]); hgA_G(X_[0]); hgA_V(X_[0])
        hgA_F(X_[1]); hgA_Fd(X_[1]); hgA_Q(X_[1])
        hgB_T(X_[0]); hgB_U(X_[0])
        for i in range(NI):
            hgB_A(X_[i])
            if i + 1 < NI:
                hgA_V(X_[i + 1])
            if i + 2 < NI:
                hgA_F(X_[i + 2])
            hgB_REC(X_[i])
            hgB_O(X_[i])
            if i + 1 < NI:
                hgB_T(X_[i + 1])
                hgB_U(X_[i + 1])
            hgB_N(X_[i])
            if i + 2 < NI:
                hgA_Fd(X_[i + 2], "dve")
            hgB_Nb(X_[i])
            if i + 2 < NI:
                hgA_Fd(X_[i + 2], "act")
            if i + 1 < NI:
                hgA_G(X_[i + 1])
                if X_[i + 1]["g"] == NGRP - 1:
                    w_release(U_HG + 4 * X_[i + 1]["hd"] + 4)
            if i + 2 < NI:
                hgA_Q(X_[i + 2])
        tap("mix", MIX[:, :, :], [128, 8, S], [mixkey(c, g) for c in range(8) for g in range(NGRP)], BF16)
        stage("mix")

        def proj_add(u, m, src, srckeyf, nk, pbase):
            sl, sk = w_slot(u)
            for g in range(NGRP):
                pb = pbase + (g % 2)
                for c in range(nk):
                    mm(PS[pb][:, :], sl[:, c * 128:(c + 1) * 128], src[:, c, g * G:(g + 1) * G], c == 0, c == nk - 1,
                       [sk, srckeyf(c, g)], [psk(pb)])
                vtt(X[:, m, g * G:(g + 1) * G], PS[pb][:, :], X[:, m, g * G:(g + 1) * G], ALU.add,
                    [psk(pb), xkey(m, g)], [xkey(m, g)])

        for m in range(8):
            proj_add(U_WO + m, m, MIX, mixkey, 8, 2 * (m % 2))
            w_release(U_WO + m + 1)
        tap("x1", X[:, :, :], [128, 8, S], [xkey(c, g) for c in range(8) for g in range(NGRP)])
        stage("x1")

        for g in range(NGRP):
            norm_T(X, xkey, 8, HT, hkey, g * G, G, g)
        XO = BIG
        xokey = lambda c, g: ("XO", c, g)
        SGBF = SGB[:, :, :, :].rearrange("p a b c -> p (a b c)")

        def xq_ap(par, dc, t0, n):
            if par == 0:
                return XQ[:, dc, t0:t0 + n]
            if dc == 0:
                return SWAX[:, 2 * S + t0:2 * S + t0 + n]
            return SGBF[:, t0:t0 + n]

        def xqkey(par, dc, g):
            return ("XQ", par, dc, g)

        C.alias([xqkey(0, dc, g) for dc in range(2) for g in range(NGRP)],
                [("QA", 0, i) for i in range(4)] + [("KA", g) for g in range(NGRP)])
        C.alias([xqkey(1, 0, g) for g in range(NGRP)], [("VA", g) for g in range(NGRP)])
        C.alias([xqkey(1, 1, g) for g in range(NGRP)], [("SGB", pp, c) for pp in range(2) for c in range(8)])
        C.alias([xokey(c, g) for c in range(8) for g in range(NGRP)], [mixkey(c, g) for c in range(8) for g in range(NGRP)])

        def xq_piece(h, dc, g):
            u = U_XQ + 2 * h + dc
            pb = g % 2
            proj_T(u, pb, g)
            acopy(xq_ap(h % 2, dc, g * G, G), PS[pb][:, :], [psk(pb)], [xqkey(h % 2, dc, g)])
            if g == NGRP - 1:
                w_release(u + 1)

        def xq_pieces(h):
            return [(lambda dc=dc, g=g: xq_piece(h, dc, g)) for dc in range(2) for g in range(NGRP)]

        for f in xq_pieces(0):
            f()
        for h in range(4):
            par = h % 2
            pend = xq_pieces(h + 1) if h + 1 < 4 else []
            for g in range(NGRP):
                t0 = g * G
                for kt in range(2):
                    pb = 2 + kt
                    for dc in range(2):
                        mm(PS[pb][:, :], XKT[:, 2 * h + dc, kt * 128:(kt + 1) * 128], xq_ap(par, dc, t0, G), dc == 0, dc == 1,
                           [("XKT", 2 * h + dc), xqkey(par, dc, g)], [psk(pb)])
                    act(SB[kt][:, :], PS[pb][:, :], AF.Exp, [psk(pb)], [sbk(kt)], scale=1.0 / 16)
                if pend:
                    pend.pop(0)()
                for kt in range(2):
                    mm(PS[4][:, :], ONES, SB[kt][:, :], kt == 0, kt == 1, ["CB", sbk(kt)], [psk(4)])
                act(SF[0][:, :], PS[4][:, :], AF.Ln, [psk(4)], [sfk(0)])
                act(SF[0][:, :], SF[0][:, :], AF.Exp, [sfk(0)], [sfk(0)], scale=-1.0)
                if pend:
                    pend.pop(0)()
                for dc in range(2):
                    pb = 5 + dc
                    for kt in range(2):
                        mm(PS[pb][:, :], XV[:, kt, h * 256 + dc * 128:h * 256 + (dc + 1) * 128], SB[kt][:, :], kt == 0, kt == 1,
                           xvkeys + [sbk(kt)], [psk(pb)])
                    vtt(XO[:, 2 * h + dc, t0:t0 + G], PS[pb][:, :], SF[0][:, :], ALU.mult, [psk(pb), sfk(0)], [xokey(2 * h + dc, g)])
            assert not pend
        for m in range(8):
            proj_add(U_XO + m, m, XO, xokey, 8, 2 * (m % 2))
            w_release(U_XO + m + 1)
        tap("x2", X[:, :, :], [128, 8, S], [xkey(c, g) for c in range(8) for g in range(NGRP)])
        stage("x2")

        def final_norm(g):
            t0 = g * G
            ss = PS[6][:, :]
            for c in range(8):
                sq = SB[6 + (c % 2)][:, :]
                act(sq, X[:, c, t0:t0 + G], AF.Square, [xkey(c, g)], [sbk(6 + (c % 2))])
                mm(ss, ONES, sq, c == 0, c == 7, [sbk(6 + (c % 2)), "CB"], [psk(6)])
            r = rsqrt_from(ss, psk(6), 7, G, 1.0 / DM)
            for c in range(8):
                vstt(X[:, c, t0:t0 + G], X[:, c, t0:t0 + G], PRM[:, 32 + c:33 + c], r, ALU.mult, ALU.mult,
                     [xkey(c, g), sfk(7), "PRM"], [xkey(c, g)])
                step = 2 if g == NGRP - 1 else 4
                if c % step == step - 1:
                    C.dma("sp", "out", yv[:, c - step + 1:c + 1, t0:t0 + G], X[:, c - step + 1:c + 1, t0:t0 + G],
                          reads=[xkey(cc, g) for cc in range(c - step + 1, c + 1)])

        for g in range(NGRP):
            norm_T(X, xkey, 24, HT, hkey, g * G, G, g)
        HID = BIG
        hidkey = lambda c, g: ("HID", c, g)
        C.alias([hidkey(c, g) for c in range(8) for g in range(NGRP)], [xokey(c, g) for c in range(8) for g in range(NGRP)])
        u = U_FFN
        for t, J in enumerate(THIRDS):
            for jj, j in enumerate(J):
                ugate, uup = u, u + 1
                for g in range(NGRP):
                    pa = 2 * (g % 2)
                    proj_T(ugate, pa, g)
                    proj_T(uup, pa + 1, g)
                    act(SF[g % 2][:, :], PS[pa][:, :], AF.Silu, [psk(pa)], [sfk(g % 2)])
                    vtt(HID[:, jj, g * G:(g + 1) * G], PS[pa + 1][:, :], SF[g % 2][:, :], ALU.mult,
                        [psk(pa + 1), sfk(g % 2)], [hidkey(jj, g)])
                u += 2
                w_release(u)
            if t < len(THIRDS) - 1:
                for m in range(8):
                    proj_add(u, m, HID, hidkey, len(J), 4)
                    u += 1
                    w_release(u)
            else:
                for g in range(NGRP):
                    for m in range(8):
                        sl, sk = w_slot(u + m)
                        pb = 4 + (m % 2)
                        for c in range(len(J)):
                            mm(PS[pb][:, :], sl[:, c * 128:(c + 1) * 128], HID[:, c, g * G:(g + 1) * G], c == 0, c == len(J) - 1,
                               [sk, hidkey(c, g)], [psk(pb)])
                        vtt(X[:, m, g * G:(g + 1) * G], PS[pb][:, :], X[:, m, g * G:(g + 1) * G], ALU.add,
                            [psk(pb), xkey(m, g)], [xkey(m, g)])
                    if g >= 1:
                        final_norm(g - 1)
                final_norm(NGRP - 1)
                u += 8
                w_release(u)
        assert u == NU

    except _Stop:
        pass
    for nm in ("out", "dbg"):
        if nm in C.dsem:
            d = C.dsem[nm]
            nc.sync.wait_ge(d[0], d[1])
    return nc, dbg_out


def _stat_unit(Wc):
    nk = Wc.shape[0] // 128
    out = np.zeros((128, 1024), np.float32)
    out[:, :nk * 128] = Wc.reshape(nk, 128, 128).transpose(1, 0, 2).reshape(128, nk * 128)
    return out


def _build_units(inp):
    w_in = np.asarray(inp["w_in"], np.float32)[0]
    w_out = np.asarray(inp["w_out"], np.float32)[0]
    w_xq = np.asarray(inp["w_xq"], np.float32)[0]
    w_xkv = np.asarray(inp["w_xkv"], np.float32)[0]
    w_xo = np.asarray(inp["w_xo"], np.float32)[0]
    w_gu = np.asarray(inp["w_gate_up"], np.float32)[0]
    w_dn = np.asarray(inp["w_down"], np.float32)[0]
    wu = np.zeros((NU, 128, 1024), np.float32)
    for m in range(8):
        wu[U_XK + m] = _stat_unit(w_xkv[:, m * 128:(m + 1) * 128])
    for hf in range(2):
        for j in range(4):
            blk = w_xkv[j * 256:(j + 1) * 256, 1024 + hf * 512:1024 + (hf + 1) * 512]
            wu[U_XV + hf * 4 + j] = blk.reshape(2, 128, 512).transpose(1, 0, 2).reshape(128, 1024)
    for hh in range(4):
        cols = np.concatenate([np.arange(hh * 64, hh * 64 + 64), np.arange((4 + hh) * 64, (4 + hh) * 64 + 64)])
        wu[U_QA + hh] = _stat_unit(w_in[:, cols])
    wu[U_KA] = _stat_unit(w_in[:, 512:640])
    wu[U_VA] = _stat_unit(w_in[:, 640:768])
    for hd in range(4):
        wu[U_HG + 4 * hd + 0] = _stat_unit(w_in[:, 1280 + hd * 128:1280 + (hd + 1) * 128])
        wu[U_HG + 4 * hd + 1] = _stat_unit(w_in[:, 768 + hd * 128:768 + (hd + 1) * 128])
        wu[U_HG + 4 * hd + 2] = _stat_unit(w_in[:, 2304 + hd * 128:2304 + (hd + 1) * 128])
        wu[U_HG + 4 * hd + 3] = _stat_unit(w_in[:, 1792 + hd * 128:1792 + (hd + 1) * 128])
    rows = []
    for c in range(4):
        rows.append(np.concatenate([np.arange(c * 64, c * 64 + 64), np.arange((4 + c) * 64, (4 + c) * 64 + 64)]))
    for c in range(4):
        rows.append(512 + c * 128 + np.arange(128))
    rows = np.concatenate(rows)
    w_out_p = w_out[rows, :]
    for m in range(8):
        wu[U_WO + m] = _stat_unit(w_out_p[:, m * 128:(m + 1) * 128])
        wu[U_XQ + m] = _stat_unit(w_xq[:, m * 128:(m + 1) * 128])
        wu[U_XO + m] = _stat_unit(w_xo[:, m * 128:(m + 1) * 128])
    for i, (kind, t, idx) in enumerate(FFN_UNITS):
        u = U_FFN + i
        if kind == "gate":
            wu[u] = _stat_unit(w_gu[:, idx * 128:(idx + 1) * 128])
        elif kind == "up":
            wu[u] = _stat_unit(w_gu[:, 2816 + idx * 128:2816 + (idx + 1) * 128])
        else:
            J = THIRDS[t]
            wu[u] = _stat_unit(w_dn[J[0] * 128:(J[-1] + 1) * 128, idx * 128:(idx + 1) * 128])
    return wu


def _build_prm(inp):
    prm = np.zeros((128, NPRM), np.float32)

    def cols(v):
        return np.asarray(v, np.float32).reshape(-1, 128).T

    prm[:, 0:8] = cols(inp["norm_mix"][0])
    prm[:, 8:16] = cols(inp["norm_xattn"][0])
    prm[:, 16:24] = cols(inp["norm_mem"][0])
    prm[:, 24:32] = cols(inp["norm_ffn"][0])
    prm[:, 32:40] = cols(inp["norm_final"])
    ag = np.asarray(inp["att_out_gain"], np.float32)[0]
    for hh in range(4):
        prm[0:64, 40 + hh] = ag[hh * 64:(hh + 1) * 64]
        prm[64:128, 40 + hh] = ag[(4 + hh) * 64:(5 + hh) * 64]
    prm[:, 44:48] = cols(inp["hg_out_gain"][0])
    lbl = np.asarray(inp["hg_lb_logits"], np.float32)
    prm[:, 48:52] = cols(lbl[0])
    prm[:, 52:56] = cols(lbl[1])
    sinks = np.asarray(inp["att_sinks"], np.float32)[0]
    for hh in range(4):
        prm[0:64, 58 + hh] = sinks[hh]
        prm[64:128, 58 + hh] = sinks[4 + hh]
    half = 8
    inv_freq = np.power(np.float32(500000.0), -np.arange(half, dtype=np.float32) * np.float32(2.0 / 16)).astype(np.float32)
    two_pi = 6.283185
    for p in range(128):
        d = p % 64
        if d < 8:
            prm[p, 56] = inv_freq[d] / np.float32(2 * np.pi)
            prm[p, 57] = -two_pi
        elif d < 16:
            prm[p, 56] = inv_freq[d - 8] / np.float32(2 * np.pi)
            prm[p, 57] = two_pi
    return prm


def _build_cst():
    cst = np.zeros((128, 6, 128), np.float32)
    k = np.arange(128)[:, None]
    q = np.arange(128)[None, :]
    cst[:, 0, :] = (k == q)
    cst[:, 1, :] = 1.0
    dq = q % 64
    cst[:, 2, :] = ((dq < 8) & (k == q + 8)) | ((dq >= 8) & (dq < 16) & (k == q - 8))
    cst[:, 3, :] = (k <= q)
    cst[:, 4, :] = (k > q)
    cst[:, 5, :] = ((k // 64) == (q // 64)) & (k <= q)
    return cst.reshape(128, 768)


_CACHE = {}


def _run(inputs, dbg=None, stop=None):
    key = (tuple(dbg or []), stop)
    if key not in _CACHE:
        _CACHE[key] = build(dbg, stop)
    nc, dbg_out = _CACHE[key]
    x = np.asarray(inputs["x"], np.float32)
    mem = np.asarray(inputs["mem"], np.float32)
    positions = np.asarray(inputs["positions"], np.int32)
    wu = _build_units(inputs)
    prm = _build_prm(inputs)
    cst = _build_cst()
    in_maps = []
    for b in range(8):
        in_maps.append({
            "xT": np.ascontiguousarray(x[b].T),
            "memT": np.ascontiguousarray(mem[b].T),
            "pos": np.ascontiguousarray(np.broadcast_to(positions[b][None, :], (128, S))),
            "wu": wu, "prm": prm, "cst": cst,
        })
    res = run_bass_kernel_spmd(nc, in_maps, core_ids=list(range(8)))
    return res


def kernel(**inputs):
    res = _run(inputs)
    y = np.stack([np.ascontiguousarray(r["yT"].T) for r in res.results], axis=0)
    return y.astype(np.float32)
```

```python
import numpy as np
import concourse.bass as bass
import concourse.mybir as mybir
from concourse.bass_utils import run_bass_kernel_spmd

F32 = mybir.dt.float32
BF16 = mybir.dt.bfloat16
I32 = mybir.dt.int32
AF = mybir.ActivationFunctionType
ALU = mybir.AluOpType

S = 2048
DM = 1024
G = 512
NGRP = 4
MEM = 256
NS = 9
EPS = 1e-6
NPRM = 64
THIRDS = [list(range(0, 8)), list(range(8, 15)), list(range(15, 22))]

U_XK = 0
U_XV = 8
U_QA = 16
U_KA = 20
U_VA = 21
U_HG = 22
U_WO = 38
U_XQ = 46
U_XO = 54
U_FFN = 62


def _ffn_units():
    lst = []
    for t, J in enumerate(THIRDS):
        for j in J:
            lst.append(("gate", t, j))
            lst.append(("up", t, j))
        for m in range(8):
            lst.append(("down", t, m))
    return lst


FFN_UNITS = _ffn_units()
NU = U_FFN + len(FFN_UNITS)


def _unit_cols(u):
    if u >= U_FFN:
        kind, t, _ = FFN_UNITS[u - U_FFN]
        if kind == "down":
            return len(THIRDS[t]) * 128
    return 1024


class _Eng:
    def __init__(self, name, eng, sem):
        self.name = name
        self.eng = eng
        self.sem = sem
        self.count = 0
        self.seen = {}
        self.pending = False


class Ctx:
    def __init__(self, nc):
        self.nc = nc
        self.E = {}
        for name, e in (("pe", nc.tensor), ("act", nc.scalar), ("dve", nc.vector),
                        ("pool", nc.gpsimd), ("sp", nc.sync)):
            self.E[name] = _Eng(name, e, nc.alloc_semaphore("s_" + name))
        self.keys = {}
        self.dsem = {}

    def _deps(self, reads, writes):
        need = {}

        def add(tok):
            sid, sem, val = tok
            if sid not in need or need[sid][1] < val:
                need[sid] = (sem, val)

        for k in reads:
            st = self.keys.get(k)
            if st is not None and st[0] is not None:
                add(st[0])
        for k in writes:
            st = self.keys.get(k)
            if st is not None:
                if st[0] is not None:
                    add(st[0])
                for t in st[1].values():
                    add(t)
        return need

    def _wait(self, es, need):
        for sid, (sem, val) in need.items():
            if sid.startswith("d_"):
                val = self.dsem[sid[2:]][1]
            if es.seen.get(sid, 0) >= val:
                continue
            if sid == es.name and es.name == "pe":
                continue
            if sid == es.name and es.pending and val > es.count:
                raise RuntimeError("wait on own pending instruction")
            es.eng.wait_ge(sem, val)
            es.seen[sid] = val

    def _record(self, tok, reads, writes):
        for k in reads:
            st = self.keys.setdefault(k, [None, {}])
            old = st[1].get(tok[0])
            if old is None or old[2] < tok[2]:
                st[1][tok[0]] = tok
        for k in writes:
            self.keys[k] = [tok, {}]

    def op(self, engname, fn, reads=(), writes=(), inc=True):
        inc = True
        psr = [k for k in reads if isinstance(k, tuple) and k[0] == "ps"]
        if psr:
            reads = [k for k in reads if k not in psr]
            writes = list(writes) + psr
        es = self.E[engname]
        self._wait(es, self._deps(reads, writes))
        ins = fn(es.eng)
        if inc:
            es.count += 1
            ins.then_inc(es.sem, 1)
            tok = (es.name, es.sem, es.count)
            es.pending = False
        else:
            tok = (es.name, es.sem, es.count + 1)
            es.pending = True
        self._record(tok, reads, writes)
        return tok

    def dma(self, queue, semname, out, in_, reads=(), writes=()):
        es = self.E[queue]
        self._wait(es, self._deps(reads, writes))
        d = self.dsem.get(semname)
        if d is None:
            d = [self.nc.alloc_semaphore("d_" + semname), 0]
            self.dsem[semname] = d
        d[1] += 16
        es.eng.dma_start(out=out, in_=in_).then_inc(d[0], 16)
        tok = ("d_" + semname, d[0], d[1])
        self._record(tok, reads, writes)
        return tok

    def alias(self, newkeys, oldkeys):
        toks = {}
        for k in oldkeys:
            st = self.keys.get(k)
            if st is None:
                continue
            cand = list(st[1].values())
            if st[0] is not None:
                cand.append(st[0])
            for t in cand:
                if t[0] not in toks or toks[t[0]][2] < t[2]:
                    toks[t[0]] = t
        for k in newkeys:
            st = self.keys.setdefault(k, [None, {}])
            for sid, t in toks.items():
                if sid not in st[1] or st[1][sid][2] < t[2]:
                    st[1][sid] = t


class _Stop(Exception):
    pass


def build(dbg=None, stop=None):
    nc = bass.Bass("TRN2", target_bir_lowering=False)
    C = Ctx(nc)
    dbg = dbg or []
    dbg_out = {}

    xT = nc.dram_tensor("xT", [DM, S], F32, kind="ExternalInput").ap()
    memT = nc.dram_tensor("memT", [DM, MEM], F32, kind="ExternalInput").ap()
    pos = nc.dram_tensor("pos", [128, S], I32, kind="ExternalInput").ap()
    wu = nc.dram_tensor("wu", [NU, 128, 1024], F32, kind="ExternalInput").ap()
    prm = nc.dram_tensor("prm", [128, NPRM], F32, kind="ExternalInput").ap()
    cst = nc.dram_tensor("cst", [128, 6 * 128], F32, kind="ExternalInput").ap()
    yT = nc.dram_tensor("yT", [DM, S], F32, kind="ExternalOutput").ap()
    yv = yT.rearrange("(c p) t -> p c t", p=128)

    X = nc.alloc_sbuf_tensor("X", [128, 8, S], F32)
    HT = nc.alloc_sbuf_tensor("HT", [128, 8, S], BF16)
    BIG = nc.alloc_sbuf_tensor("BIG", [128, 8, S], BF16)
    RING = nc.alloc_sbuf_tensor("RING", [128, NS, 1024], BF16)
    XKT = nc.alloc_sbuf_tensor("XKT", [128, 8, MEM], BF16)
    XV = nc.alloc_sbuf_tensor("XV", [128, 2, DM], BF16)
    PRM = nc.alloc_sbuf_tensor("PRM", [128, NPRM], F32)
    CB = nc.alloc_sbuf_tensor("CB", [128, 6, 128], BF16)
    LBT = nc.alloc_sbuf_tensor("LBT", [128, 20], F32)
    NSF = 10
    NSB = 11
    SF = [nc.alloc_sbuf_tensor("SF%d" % i, [128, G], F32) for i in range(NSF)]
    SB = [nc.alloc_sbuf_tensor("SB%d" % i, [128, G], BF16) for i in range(NSB)]
    SWAX = nc.alloc_sbuf_tensor("SWAX", [128, 3 * S], BF16)
    QA = SWAX[:, 0:S].rearrange("p (a b) -> p a b", a=4)
    KA = SWAX[:, S:2 * S]
    VA = SWAX[:, 2 * S:3 * S].rearrange("p (a b) -> p a b", a=16)
    XQ = SWAX[:, 0:2 * S].rearrange("p (a b) -> p a b", a=2)
    SGB = nc.alloc_sbuf_tensor("SGB", [128, 2, 8, 128], BF16)
    ZB = nc.alloc_sbuf_tensor("ZB", [128, 9, 128], F32)
    EBL = nc.alloc_sbuf_tensor("EBL", [128, 2, 8], F32)
    ES = nc.alloc_sbuf_tensor("ES", [128, 4], F32)
    LN2C = nc.alloc_sbuf_tensor("LN2C", [128, 1], F32)
    PS = [nc.alloc_psum_tensor("PS%d" % i, [128, G], F32) for i in range(7)]
    PST = nc.alloc_psum_tensor("PST", [128, 2 * G], BF16)

    IDENT = CB[:, 0, :]
    ONES = CB[:, 1, :]
    PSW = CB[:, 2, :]
    MCUR = CB[:, 3, :]
    MPREV = CB[:, 4, :]
    MBLK = CB[:, 5, :]

    def psk(i):
        return ("ps", i)

    def sfk(i):
        return ("sf", i)

    def sbk(i):
        return ("sb", i)

    class WS:
        issued = 0
        rel = 0

    def w_pump():
        while WS.issued < min(NU, WS.rel + NS):
            u = WS.issued
            sl = u % NS
            ncol = _unit_cols(u)
            C.dma("pool", "ring%d" % sl, RING[:, sl, 0:ncol], wu[u, :, 0:ncol], writes=[("ring", sl)])
            WS.issued += 1

    def w_slot(u):
        assert u < WS.issued and u >= WS.rel, (u, WS.issued, WS.rel)
        return RING[:, u % NS, :], ("ring", u % NS)

    def w_release(upto):
        if upto > WS.rel:
            WS.rel = upto
        w_pump()

    def mm(out, lhsT, rhs, start, stop, reads, writes, inc=None):
        if inc is None:
            inc = stop
        return C.op("pe", lambda e: e.matmul(out, lhsT=lhsT, rhs=rhs, start=start, stop=stop),
                    reads=reads, writes=writes, inc=inc)

    def act(out, in_, func, reads, writes, scale=1.0, bias=0.0):
        return C.op("act", lambda e: e.activation(out=out, in_=in_, func=func, scale=scale, bias=bias),
                    reads=reads, writes=writes)

    def acopy(out, in_, reads, writes):
        return C.op("act", lambda e: e.copy(out=out, in_=in_), reads=reads, writes=writes)

    def vtt(out, in0, in1, op, reads, writes, eng="dve"):
        return C.op(eng, lambda e: e.tensor_tensor(out=out, in0=in0, in1=in1, op=op), reads=reads, writes=writes)

    def vts(out, in0, s1, s2, op0, op1, reads, writes, eng="dve"):
        if s2 is None:
            return C.op(eng, lambda e: e.tensor_scalar(out=out, in0=in0, scalar1=s1, scalar2=None, op0=op0),
                        reads=reads, writes=writes)
        return C.op(eng, lambda e: e.tensor_scalar(out=out, in0=in0, scalar1=s1, scalar2=s2, op0=op0, op1=op1),
                    reads=reads, writes=writes)

    def vstt(out, in0, scalar, in1, op0, op1, reads, writes):
        return C.op("dve", lambda e: e.scalar_tensor_tensor(out=out, in0=in0, scalar=scalar, in1=in1, op0=op0, op1=op1),
                    reads=reads, writes=writes)

    def vcopy(out, in_, reads, writes, eng="dve"):
        return C.op(eng, lambda e: e.tensor_copy(out=out, in_=in_), reads=reads, writes=writes)

    def vrecip(out, in_, reads, writes):
        return C.op("dve", lambda e: e.reciprocal(out=out, in_=in_), reads=reads, writes=writes)

    def rsqrt_from(ps_ap, pskey, sf_i, ncol, inv_n):
        r = SF[sf_i][:, 0:ncol]
        act(r, ps_ap, AF.Ln, [pskey], [sfk(sf_i)], scale=inv_n, bias=EPS)
        act(r, r, AF.Exp, [sfk(sf_i)], [sfk(sf_i)], scale=-0.5)
        return r

    def tap(name, ap, shape, keys, dtype=F32):
        if name not in dbg:
            return
        d = nc.dram_tensor("dbg_" + name, list(shape), F32, kind="ExternalOutput").ap()
        dbg_out[name] = d
        C.dma("pool" if dtype != F32 else "sp", "dbg", d, ap, reads=keys)

    def stage(name):
        if stop == name:
            raise _Stop()

    try:
        C.dma("sp", "prm", PRM[:, :], prm[:, :], writes=["PRM"])
        C.dma("pool", "cst", CB[:, :, :].rearrange("p a b -> p (a b)"), cst[:, :], writes=["CB"])
        w_pump()
        MEMX = BIG[:, 0:2, :].rearrange("p a b -> p (a b)").bitcast(F32).rearrange("p (c t) -> p c t", c=8)
        MN = BIG[:, 2, 0:2048].rearrange("p (c t) -> p c t", c=8)
        for c in range(8):
            C.dma("sp", "mem", MEMX[:, c, :], memT[c * 128:(c + 1) * 128, :], writes=[("MEMX", c)])
        xv = xT.rearrange("(c p) t -> p c t", p=128)
        for g in range(NGRP):
            for h4 in range(2):
                C.dma("sp", "x%d" % g, X[:, 4 * h4:4 * h4 + 4, g * G:(g + 1) * G], xv[:, 4 * h4:4 * h4 + 4, g * G:(g + 1) * G],
                      writes=[("X", c, g) for c in range(4 * h4, 4 * h4 + 4)])

        vtt(LBT[:, 0:4], PRM[:, 48:52], PRM[:, 52:56], ALU.subtract, ["PRM"], ["LB0"])
        act(LBT[:, 0:4], LBT[:, 0:4], AF.Tanh, ["LB0"], ["LB0"], scale=0.5)
        vts(LBT[:, 4:8], LBT[:, 0:4], -0.25, 0.25, ALU.mult, ALU.add, ["LB0"], ["LB1"])
        vts(LBT[:, 8:12], LBT[:, 0:4], 0.25, 0.75, ALU.mult, ALU.add, ["LB0"], ["LB2"])
        vts(LBT[:, 12:16], LBT[:, 0:4], 0.25, -0.25, ALU.mult, ALU.add, ["LB0"], ["LB3"])
        vts(LBT[:, 16:20], LBT[:, 0:4], -0.25, 0.25, ALU.mult, ALU.add, ["LB0"], ["LB4"])
        act(ES[:, :], PRM[:, 58:62], AF.Exp, ["PRM"], ["ES"])
        C.op("dve", lambda e: e.memset(LN2C[:, :], 20.79441541679836), writes=["LN2C"])

        stage("setup")
        def norm_T(src, srckey, gcol, dst, dstkey, t0, ncol, gi):
            ss = PS[6][:, 0:ncol]
            for c in range(8):
                sq = SB[6 + (c % 2)][:, 0:ncol]
                act(sq, src[:, c, t0:t0 + ncol], AF.Square, [srckey(c, gi)], [sbk(6 + (c % 2))])
                mm(ss, ONES, sq, c == 0, c == 7, [sbk(6 + (c % 2)), "CB"], [psk(6)])
            r = rsqrt_from(ss, psk(6), 7, ncol, 1.0 / DM)
            for c in range(8):
                vstt(dst[:, c, t0:t0 + ncol], src[:, c, t0:t0 + ncol], PRM[:, gcol + c:gcol + c + 1], r,
                     ALU.mult, ALU.mult, [srckey(c, gi), sfk(7), "PRM"], [dstkey(c, gi)])

        xkey = lambda c, g: ("X", c, g)
        hkey = lambda c, g: ("HT", c, g)

        norm_T(MEMX, lambda c, g: ("MEMX", c), 16, MN, lambda c, g: ("MN", c), 0, MEM, 0)
        mnkeys = [("MN", c) for c in range(8)]
        tap("mn", MN, [128, 8, MEM], mnkeys, BF16)
        stage("mn")
        for m in range(8):
            sl, sk = w_slot(U_XK + m)
            pb = m % 2
            for c in range(8):
                mm(PS[pb][:, 0:MEM], sl[:, c * 128:(c + 1) * 128], MN[:, c, :], c == 0, c == 7,
                   [sk, ("MN", c)], [psk(pb)])
            acopy(XKT[:, m, :], PS[pb][:, 0:MEM], [psk(pb)], [("XKT", m)])
            w_release(U_XK + m + 1)
        for hf in range(2):
            for tt in range(2):
                pb = 2 + tt
                for c in range(8):
                    sl, sk = w_slot(U_XV + hf * 4 + c // 2)
                    mm(PS[pb][:, :], MN[:, c, tt * 128:(tt + 1) * 128], sl[:, (c % 2) * 512:(c % 2 + 1) * 512],
                       c == 0, c == 7, [sk, ("MN", c)], [psk(pb)])
                acopy(XV[:, tt, hf * 512:(hf + 1) * 512], PS[pb][:, :], [psk(pb)], [("XV", tt, hf)])
            w_release(U_XV + hf * 4 + 4)
        xvkeys = [("XV", tt, hf) for tt in range(2) for hf in range(2)]
        tap("xkt", XKT[:, :, :], [128, 8, MEM], [("XKT", m) for m in range(8)], BF16)
        tap("xv", XV[:, :, :], [128, 2, DM], xvkeys, BF16)
        stage("xv")

        for g in range(NGRP):
            norm_T(X, xkey, 0, HT, hkey, g * G, G, g)
        tap("ht", HT[:, :, :], [128, 8, S], [hkey(c, g) for c in range(8) for g in range(NGRP)], BF16)
        stage("ht")

        MIX = BIG
        mixkey = lambda c, g: ("MIX", c, g)
        C.alias([mixkey(c, g) for c in range(8) for g in range(NGRP)],
                [("MEMX", c) for c in range(8)] + mnkeys)

        def proj_T(u, pb, g):
            sl, sk = w_slot(u)
            for c in range(8):
                mm(PS[pb][:, :], sl[:, c * 128:(c + 1) * 128], HT[:, c, g * G:(g + 1) * G], c == 0, c == 7,
                   [sk, hkey(c, g)], [psk(pb)])

        TWO_PI = 6.283185

        def qakey(bi, i):
            return ("QA", bi, i)

        def swa_tables(g):
            t0 = g * G
            ci, si = 0, 1
            posi = SF[2][:, :].bitcast(I32)
            C.dma("sp", "pos", posi, pos[:, t0:t0 + G], writes=[sfk(2)])
            vcopy(SF[3][:, :], posi, [sfk(2)], [sfk(3)])
            vts(SF[3][:, :], SF[3][:, :], PRM[:, 56:57], None, ALU.mult, None, [sfk(3), "PRM"], [sfk(3)])
            ni = SF[2][:, :].bitcast(I32)
            vcopy(ni, SF[3][:, :], [sfk(3)], [sfk(2)])
            vcopy(SF[4][:, :], ni, [sfk(2)], [sfk(4)])
            vtt(SF[4][:, :], SF[3][:, :], SF[4][:, :], ALU.subtract, [sfk(3), sfk(4)], [sfk(4)])
            act(SF[si][:, :], SF[4][:, :], AF.Sin, [sfk(4), "PRM"], [sfk(si)], scale=PRM[:, 57:58])
            vts(SF[3][:, :], SF[3][:, :], 0.25, None, ALU.add, None, [sfk(3)], [sfk(3)])
            vcopy(ni, SF[3][:, :], [sfk(3)], [sfk(2)])
            vcopy(SF[4][:, :], ni, [sfk(2)], [sfk(4)])
            vtt(SF[4][:, :], SF[3][:, :], SF[4][:, :], ALU.subtract, [sfk(3), sfk(4)], [sfk(4)])
            act(SF[ci][:, :], SF[4][:, :], AF.Sin, [sfk(4)], [sfk(ci)], scale=TWO_PI)

        def swa_proj(g, i):
            t0 = g * G
            ci, si = 0, 1
            pb = 2 + 2 * (i % 2)
            proj_T(U_QA + i, pb, g)
            acopy(SB[7 + (i % 2)][:, :], PS[pb][:, :], [psk(pb)], [sbk(7 + (i % 2))])
            mm(PS[pb + 1][:, :], PSW, SB[7 + (i % 2)][:, :], True, True, [sbk(7 + (i % 2)), "CB"], [psk(pb + 1)])
            t1, t2 = SF[2 + 2 * (i % 2)], SF[3 + 2 * (i % 2)]
            k1, k2 = sfk(2 + 2 * (i % 2)), sfk(3 + 2 * (i % 2))
            vtt(t1[:, :], PS[pb][:, :], SF[ci][:, :], ALU.mult, [psk(pb), sfk(ci)], [k1])
            vtt(t2[:, :], PS[pb + 1][:, :], SF[si][:, :], ALU.mult, [psk(pb + 1), sfk(si)], [k2])
            if i < 4:
                vtt(QA[:, i, :], t1[:, :], t2[:, :], ALU.add, [k1, k2], [qakey(0, i)])
            else:
                vtt(KA[:, t0:t0 + G], t1[:, :], t2[:, :], ALU.add, [k1, k2], [("KA", g)])

        def swa_v(g):
            t0 = g * G
            sl, sk = w_slot(U_VA)
            for tt in range(4):
                for c in range(8):
                    mm(PS[6][:, tt * 128:(tt + 1) * 128], HT[:, c, t0 + tt * 128:t0 + (tt + 1) * 128],
                       sl[:, c * 128:(c + 1) * 128], c == 0, c == 7, [sk, hkey(c, g)], [psk(6)])
            acopy(VA[:, 4 * g:4 * g + 4, :].rearrange("p a b -> p (a b)"), PS[6][:, :], [psk(6)], [("VA", g)])

        PT_IDX = [[0, 1, 2, 3], [7, 8, 9, 10]]
        OD_BANK = [(2, 3), (4, 5)]
        TAIL_SF = [(7, 8), (5, 6)]

        def swa_front_parts(n):
            g, j = divmod(n, 4)
            st = n % 2
            q0 = j * 128
            kts = [1] if n == 0 else [0, 1]
            ob, db = OD_BANK[st]
            pt = {}
            combos = [(kt, gg) for kt in kts for gg in range(2)]

            def sc(idx):
                kt, gg = combos[idx]
                pr = slice(gg * 64, (gg + 1) * 64)
                kb = n - 1 + kt
                sb_i = idx % 2
                mm(PS[sb_i][:, :].rearrange("p (a b) -> p a b", a=4), KA[pr, kb * 128:(kb + 1) * 128],
                   QA[pr, :, q0:q0 + 128], True, True,
                   [("KA", kb // 4)] + [qakey(0, i) for i in range(4)], [psk(sb_i)])
                et = SB[4 + (idx % 2)]
                act(et[:, :], PS[sb_i][:, :], AF.Exp, [psk(sb_i)], [sbk(4 + (idx % 2))], scale=0.125)
                msk = MCUR if kt == 1 else MPREV
                pi = PT_IDX[st][idx]
                vtt(SB[pi][:, :].rearrange("p (a b) -> p a b", a=4), et[:, :].rearrange("p (a b) -> p a b", a=4),
                    msk.unsqueeze(1).broadcast_to([128, 4, 128]), ALU.mult,
                    [sbk(4 + (idx % 2)), "CB"], [sbk(pi)])
                pt[(gg, kt)] = pi

            def part1():
                for idx in range(min(2, len(combos))):
                    sc(idx)

            def part2():
                for idx in range(2, len(combos)):
                    sc(idx)

            def part3():
                for ii, kt in enumerate(kts):
                    kb = n - 1 + kt
                    for gg in range(2):
                        pr = slice(gg * 64, (gg + 1) * 64)
                        mm(PS[ob][pr, :], VA[:, kb, gg * 64:(gg + 1) * 64], SB[pt[(gg, kt)]][:, :], ii == 0, ii == len(kts) - 1,
                           [("VA", kb // 4), sbk(pt[(gg, kt)])], [psk(ob)])
                for ii, kt in enumerate(kts):
                    for gg in range(2):
                        pr = slice(gg * 64, (gg + 1) * 64)
                        mm(PS[db][pr, :], ONES[:, 0:64], SB[pt[(gg, kt)]][:, :], ii == 0, ii == len(kts) - 1,
                           ["CB", sbk(pt[(gg, kt)])], [psk(db)])

            return part1, part2, part3

        def swa_tail_parts(n):
            g, j = divmod(n, 4)
            st = n % 2
            t0 = g * G
            q0 = j * 128
            ob, db = OD_BANK[st]
            fd, fa = TAIL_SF[st]

            def part1():
                for hh in range(4):
                    act(SF[fd][:, hh * 128:(hh + 1) * 128], PS[db][:, hh * 128:(hh + 1) * 128], AF.Ln, [psk(db), "ES"], [sfk(fd)],
                        bias=ES[:, hh:hh + 1])
                act(SF[fd][:, :], SF[fd][:, :], AF.Exp, [sfk(fd)], [sfk(fd)], scale=-1.0)
                vtt(SF[fa][:, :], PS[ob][:, :], SF[fd][:, :], ALU.mult, [psk(ob), sfk(fd)], [sfk(fa)])

            def part2():
                act(SB[6][:, :], SF[fa][:, :], AF.Square, [sfk(fa)], [sbk(6)])
                for hh in range(4):
                    mm(PS[6][:, 0:128], ONES, SB[6][:, hh * 128:(hh + 1) * 128], hh == 0, hh == 3, [sbk(6), "CB"], [psk(6)])

            def part3():
                r = rsqrt_from(PS[6][:, 0:128], psk(6), 9, 128, 1.0 / 512)
                for hh in range(4):
                    vstt(MIX[:, hh, t0 + q0:t0 + q0 + 128], SF[fa][:, hh * 128:(hh + 1) * 128], PRM[:, 40 + hh:41 + hh], r,
                         ALU.mult, ALU.mult, [sfk(fa), sfk(9), "PRM"], [mixkey(hh, g)])

            return part1, part2, part3

        def swa_blocks(n0):
            F = [swa_front_parts(n0 + k) for k in range(4)]
            T = [swa_tail_parts(n0 + k) for k in range(4)]
            for k in range(2):
                for f in F[k]:
                    f()
            for k in range(4):
                nf = F[k + 2] if k + 2 < 4 else (lambda: None, lambda: None, lambda: None)
                T[k][0]()
                nf[0]()
                T[k][1]()
                nf[1]()
                T[k][2]()
                nf[2]()

        for g in range(NGRP):
            swa_tables(g)
            for i in range(5):
                swa_proj(g, i)
            swa_v(g)
            if g == 0:
                tap("qa0", QA[:, :, :], [128, 4, G], [qakey(0, i) for i in range(4)], BF16)
            n0 = 4 * g
            swa_blocks(n0)
        w_release(U_VA + 1)
        tap("mixa", MIX[:, 0:4, :], [128, 4, S], [mixkey(c, g) for c in range(4) for g in range(NGRP)], BF16)
        stage("mixa")

        QTb = [SB[0], SB[1]]
        KTb = [SB[2], SB[3]]
        KDb = [SB[4], SB[5]]
        VRb = [SB[6], SB[7]]
        I_KDT, I_AM, I_SQO = 8, 9, 10
        SGTb = [SF[5], SF[6]]

        def hg_ctx(hd, g, bi):
            d = dict(hd=hd, g=g, bi=bi, t0=g * G, par=g % 2,
                     qt=QTb[bi], kt=KTb[bi], kd=KDb[bi], vr=VRb[bi],
                     kq=sbk(bi), kk=sbk(2 + bi), kkd=sbk(4 + bi), kv=sbk(6 + bi))
            d["uf"], d["uq"], d["ug"], d["ui"] = (U_HG + 4 * hd + k for k in range(4))
            d["ik"] = 0 if bi == 0 else 9
            return d

        def hgA_F(x):
            hd, g, bi, ik = x["hd"], x["g"], x["bi"], x["ik"]
            proj_T(x["uf"], 0, g)
            act(SF[ik][:, :], PS[0][:, :], AF.Tanh, [psk(0)], [sfk(ik)], scale=0.5)
            act(SF[3][:, :], SF[ik][:, :], AF.Identity, [sfk(ik), "LB1", "LB2"], [sfk(3)],
                scale=LBT[:, 4 + hd:5 + hd], bias=LBT[:, 8 + hd:9 + hd])
            act(SF[ik][:, :], SF[ik][:, :], AF.Identity, [sfk(ik), "LB3", "LB4"], [sfk(ik)],
                scale=LBT[:, 12 + hd:13 + hd], bias=LBT[:, 16 + hd:17 + hd])

        def hgA_Fd(x, part="all"):
            bi = x["bi"]
            P3 = SF[1][:, :].rearrange("p (c s) -> p c s", s=64)
            if part in ("all", "dve"):
                F3 = SF[3][:, :].rearrange("p (c s) -> p c s", s=64)
                Z3 = SF[8][:, :].rearrange("p (c s) -> p c s", s=64)
                vcopy(Z3[:, :, 0], F3[:, :, 0], [sfk(3)], [sfk(8)])
                C.op("dve", lambda e: e.tensor_tensor_scan(out=SF[1][:, :], data0=SF[3][:, :], data1=SF[8][:, :], initial=1.0,
                                                           op0=ALU.mult, op1=ALU.max), reads=[sfk(3), sfk(8)], writes=[sfk(1)])
                vcopy(EBL[:, bi, :], P3[:, :, 63], [sfk(1)], [("EBL", bi)])
            if part in ("all", "act"):
                act(SF[2][:, :], SF[1][:, :], AF.Ln, [sfk(1)], [sfk(2)], scale=float(2 ** 30))
                act(SF[2][:, :], SF[2][:, :], AF.Exp, [sfk(2), "LN2C"], [sfk(2)], scale=-1.0, bias=LN2C[:, 0:1])

        def hgA_Q(x):
            g = x["g"]
            P3 = SF[1][:, :].rearrange("p (c s) -> p c s", s=64)
            proj_T(x["uq"], 1, g)
            act(SF[4][:, :], PS[1][:, :], AF.Silu, [psk(1)], [sfk(4)])
            vtt(x["kt"][:, :], SF[x["ik"]][:, :], SF[2][:, :], ALU.mult, [sfk(x["ik"]), sfk(2)], [x["kk"]], eng="pool")
            vtt(x["qt"][:, :], SF[4][:, :], SF[1][:, :], ALU.mult, [sfk(4), sfk(1)], [x["kq"]], eng="pool")

        def hgA_G(x):
            proj_T(x["ug"], 2, x["g"])
            act(SGTb[x["bi"]][:, :], PS[2][:, :], AF.Silu, [psk(2)], [sfk(5 + x["bi"])])

        def hgA_V(x):
            g, t0 = x["g"], x["t0"]
            sl, sk = w_slot(x["ui"])
            for tt in range(4):
                for c in range(8):
                    mm(PS[3][:, tt * 128:(tt + 1) * 128], HT[:, c, t0 + tt * 128:t0 + (tt + 1) * 128],
                       sl[:, c * 128:(c + 1) * 128], c == 0, c == 7, [sk, hkey(c, g)], [psk(3)])
            acopy(x["vr"][:, :], PS[3][:, :], [psk(3)], [x["kv"]])

        hg_state = {"sidx": 0}
        KDT, AM, SQO = SB[I_KDT], SB[I_AM], SB[I_SQO]

        def hgB_T(x):
            if x["g"] == 0:
                C.op("dve", lambda e: e.memset(ZB[:, 0, :], 0.0), writes=[("ZB", 0)])
            kd = x["kt"]
            for j in range(4):
                C.op("pe", lambda e, j=j: e.transpose(out=PST[:, j * 128:(j + 1) * 128], in_=kd[:, j * 128:(j + 1) * 128],
                                                     identity=IDENT), reads=[x["kk"], "CB"], writes=[psk(7)])
            acopy(KDT[:, :], PST[:, 0:512], [psk(7)], [sbk(I_KDT)])

        def hgB_U(x):
            vr = x["vr"]
            for c in range(8):
                j, hf = divmod(c, 2)
                pr = slice(hf * 64, (hf + 1) * 64)
                ub = 4 + hf
                mm(PS[ub][:, j * 128:(j + 1) * 128], KDT[pr, j * 128:(j + 1) * 128], vr[pr, j * 128:(j + 1) * 128],
                   True, True, [sbk(I_KDT), x["kv"]], [psk(ub)])

        def hgB_A(x):
            qt, kt = x["qt"], x["kt"]
            for j in range(4):
                mm(PS[6][:, j * 128:(j + 1) * 128], kt[:, j * 128:(j + 1) * 128], qt[:, j * 128:(j + 1) * 128],
                   True, True, [x["kk"], x["kq"]], [psk(6)])
            vtt(AM[:, :].rearrange("p (a b) -> p a b", a=4), PS[6][:, :].rearrange("p (a b) -> p a b", a=4),
                MBLK.unsqueeze(1).broadcast_to([128, 4, 128]), ALU.mult, [psk(6), "CB"], [sbk(I_AM)])

        def hgB_REC(x):
            bi, par = x["bi"], x["par"]
            for c in range(8):
                ub = 4 + c % 2
                vstt(ZB[:, c + 1, :], ZB[:, c, :], 1.0 if c == 0 else EBL[:, bi, c - 1:c],
                     PS[ub][:, (c // 2) * 128:(c // 2 + 1) * 128], ALU.mult, ALU.add,
                     [("ZB", c), ("EBL", bi), psk(ub)], [("ZB", c + 1)])
            for hf in range(2):
                cs = range(4 * hf, 4 * hf + 4)
                vtt(SGB[:, par, 4 * hf:4 * hf + 4, :], ZB[:, 1 + 4 * hf:5 + 4 * hf, :],
                    EBL[:, bi, 4 * hf:4 * hf + 4].unsqueeze(2).broadcast_to([128, 4, 128]), ALU.mult,
                    [("ZB", c + 1) for c in cs] + [("EBL", bi)], [("SGB", par, c) for c in cs])
            vts(ZB[:, 0, :], ZB[:, 8, :], EBL[:, bi, 7:8], None, ALU.mult, None,
                [("ZB", 8), ("EBL", bi)], [("ZB", 0)])

        def hgB_O(x):
            g, par, qt, vr = x["g"], x["par"], x["qt"], x["vr"]
            for j in range(4):
                inter = []
                for hf in range(2):
                    c = 2 * j + hf
                    if c == 0:
                        if g == 0:
                            continue
                        inter.append((c, SGB[:, 1 - par, 7, :], ("SGB", 1 - par, 7)))
                    else:
                        inter.append((c, SGB[:, par, c - 1, :], ("SGB", par, c - 1)))
                mm(PS[6][:, j * 128:(j + 1) * 128], vr[:, j * 128:(j + 1) * 128], AM[:, j * 128:(j + 1) * 128],
                   True, len(inter) == 0, [x["kv"], sbk(I_AM)], [psk(6)])
                for ii, (c, st, stk) in enumerate(inter):
                    mm(PS[6][:, c * 64:(c + 1) * 64], st, qt[:, c * 64:(c + 1) * 64], False, ii == len(inter) - 1,
                       [stk, x["kq"]], [psk(6)])

        def hgB_N(x):
            hd, g, bi, t0 = x["hd"], x["g"], x["bi"], x["t0"]
            act(SQO[:, :], PS[6][:, :], AF.Square, [psk(6)], [sbk(I_SQO)])
            mm(PS[3][:, :], ONES, SQO[:, :], True, True, [sbk(I_SQO), "CB"], [psk(3)])

        def hgB_Nb(x):
            hd, g, bi, t0 = x["hd"], x["g"], x["bi"], x["t0"]
            r = rsqrt_from(PS[3][:, :], psk(3), 7, G, 1.0 / 128)
            vtt(SF[7][:, :], PS[6][:, :], r, ALU.mult, [psk(6), sfk(7)], [sfk(7)])
            vstt(MIX[:, 4 + hd, t0:t0 + G], SF[7][:, :], PRM[:, 44 + hd:45 + hd], SGTb[bi][:, :], ALU.mult, ALU.mult,
                 [sfk(7), sfk(5 + bi), "PRM"], [mixkey(4 + hd, g)])

        C.op("dve", lambda e: e.memset(SF[8][:, :], 0.0), writes=[sfk(8)])
        its = [(hd, g) for hd in range(4) for g in range(NGRP)]
        X_ = [hg_ctx(hd, g, i % 2) for i, (hd, g) in enumerate(its)]
        NI = len(its)
        hgA_F(X_[0]); hgA_Fd(X_[0]); hgA_Q(X_[0]); hgA_G(X_[0]); hgA_V(X_[0])
        hgA_F(X_[1]); hgA_Fd(X_[1]); hgA_Q(X_[1])
        hgB_T(X_[0]); hgB_U(X_[0])
        for i in range(NI):
            hgB_A(X_[i])
            if i + 1 < NI:
                hgA_V(X_[i + 1])
            if i + 2 < NI:
                hgA_F(X_[i + 2])
            hgB_REC(X_[i])
            hgB_O(X_[i])
            if i + 1 < NI:
                hgB_T(X_[i + 1])
                hgB_U(X_[i + 1])
            hgB_N(X_[i])
            if i + 2 < NI:
                hgA_Fd(X_[i + 2], "dve")
            hgB_Nb(X_[i])
            if i + 2 < NI:
                hgA_Fd(X_[i + 2], "act")
            if i + 1 < NI:
                hgA_G(X_[i + 1])
                if X_[i + 1]["g"] == NGRP - 1:
                    w_release(U_HG + 4 * X_[i + 1]["hd"] + 4)
            if i + 2 < NI:
                hgA_Q(X_[i + 2])
        tap("mix", MIX[:, :, :], [128, 8, S], [mixkey(c, g) for c in range(8) for g in range(NGRP)], BF16)
        stage("mix")

        def proj_add(u, m, src, srckeyf, nk, pbase):
            sl, sk = w_slot(u)
            for g in range(NGRP):
                pb = pbase + (g % 2)
                for c in range(nk):
                    mm(PS[pb][:, :], sl[:, c * 128:(c + 1) * 128], src[:, c, g * G:(g + 1) * G], c == 0, c == nk - 1,
                       [sk, srckeyf(c, g)], [psk(pb)])
                vtt(X[:, m, g * G:(g + 1) * G], PS[pb][:, :], X[:, m, g * G:(g + 1) * G], ALU.add,
                    [psk(pb), xkey(m, g)], [xkey(m, g)])

        for m in range(8):
            proj_add(U_WO + m, m, MIX, mixkey, 8, 2 * (m % 2))
            w_release(U_WO + m + 1)
        tap("x1", X[:, :, :], [128, 8, S], [xkey(c, g) for c in range(8) for g in range(NGRP)])
        stage("x1")

        for g in range(NGRP):
            norm_T(X, xkey, 8, HT, hkey, g * G, G, g)
        XO = BIG
        xokey = lambda c, g: ("XO", c, g)
        SGBF = SGB[:, :, :, :].rearrange("p a b c -> p (a b c)")

        def xq_ap(par, dc, t0, n):
            if par == 0:
                return XQ[:, dc, t0:t0 + n]
            if dc == 0:
                return SWAX[:, 2 * S + t0:2 * S + t0 + n]
            return SGBF[:, t0:t0 + n]

        def xqkey(par, dc, g):
            return ("XQ", par, dc, g)

        C.alias([xqkey(0, dc, g) for dc in range(2) for g in range(NGRP)],
                [("QA", 0, i) for i in range(4)] + [("KA", g) for g in range(NGRP)])
        C.alias([xqkey(1, 0, g) for g in range(NGRP)], [("VA", g) for g in range(NGRP)])
        C.alias([xqkey(1, 1, g) for g in range(NGRP)], [("SGB", pp, c) for pp in range(2) for c in range(8)])
        C.alias([xokey(c, g) for c in range(8) for g in range(NGRP)], [mixkey(c, g) for c in range(8) for g in range(NGRP)])

        def xq_piece(h, dc, g):
            u = U_XQ + 2 * h + dc
            pb = g % 2
            proj_T(u, pb, g)
            acopy(xq_ap(h % 2, dc, g * G, G), PS[pb][:, :], [psk(pb)], [xqkey(h % 2, dc, g)])
            if g == NGRP - 1:
                w_release(u + 1)

        def xq_pieces(h):
            return [(lambda dc=dc, g=g: xq_piece(h, dc, g)) for dc in range(2) for g in range(NGRP)]

        for f in xq_pieces(0):
            f()
        for h in range(4):
            par = h % 2
            pend = xq_pieces(h + 1) if h + 1 < 4 else []
            for g in range(NGRP):
                t0 = g * G
                for kt in range(2):
                    pb = 2 + kt
                    for dc in range(2):
                        mm(PS[pb][:, :], XKT[:, 2 * h + dc, kt * 128:(kt + 1) * 128], xq_ap(par, dc, t0, G), dc == 0, dc == 1,
                           [("XKT", 2 * h + dc), xqkey(par, dc, g)], [psk(pb)])
                    act(SB[kt][:, :], PS[pb][:, :], AF.Exp, [psk(pb)], [sbk(kt)], scale=1.0 / 16)
                if pend:
                    pend.pop(0)()
                for kt in range(2):
                    mm(PS[4][:, :], ONES, SB[kt][:, :], kt == 0, kt == 1, ["CB", sbk(kt)], [psk(4)])
                act(SF[0][:, :], PS[4][:, :], AF.Ln, [psk(4)], [sfk(0)])
                act(SF[0][:, :], SF[0][:, :], AF.Exp, [sfk(0)], [sfk(0)], scale=-1.0)
                if pend:
                    pend.pop(0)()
                for dc in range(2):
                    pb = 5 + dc
                    for kt in range(2):
                        mm(PS[pb][:, :], XV[:, kt, h * 256 + dc * 128:h * 256 + (dc + 1) * 128], SB[kt][:, :], kt == 0, kt == 1,
                           xvkeys + [sbk(kt)], [psk(pb)])
                    vtt(XO[:, 2 * h + dc, t0:t0 + G], PS[pb][:, :], SF[0][:, :], ALU.mult, [psk(pb), sfk(0)], [xokey(2 * h + dc, g)])
            assert not pend
        for m in range(8):
            proj_add(U_XO + m, m, XO, xokey, 8, 2 * (m % 2))
            w_release(U_XO + m + 1)
        tap("x2", X[:, :, :], [128, 8, S], [xkey(c, g) for c in range(8) for g in range(NGRP)])
        stage("x2")

        def final_norm(g):
            t0 = g * G
            ss = PS[6][:, :]
            for c in range(8):
                sq = SB[6 + (c % 2)][:, :]
                act(sq, X[:, c, t0:t0 + G], AF.Square, [xkey(c, g)], [sbk(6 + (c % 2))])
                mm(ss, ONES, sq, c == 0, c == 7, [sbk(6 + (c % 2)), "CB"], [psk(6)])
            r = rsqrt_from(ss, psk(6), 7, G, 1.0 / DM)
            for c in range(8):
                vstt(X[:, c, t0:t0 + G], X[:, c, t0:t0 + G], PRM[:, 32 + c:33 + c], r, ALU.mult, ALU.mult,
                     [xkey(c, g), sfk(7), "PRM"], [xkey(c, g)])
                step = 2 if g == NGRP - 1 else 4
                if c % step == step - 1:
                    C.dma("sp", "out", yv[:, c - step + 1:c + 1, t0:t0 + G], X[:, c - step + 1:c + 1, t0:t0 + G],
                          reads=[xkey(cc, g) for cc in range(c - step + 1, c + 1)])

        for g in range(NGRP):
            norm_T(X, xkey, 24, HT, hkey, g * G, G, g)
        HID = BIG
        hidkey = lambda c, g: ("HID", c, g)
        C.alias([hidkey(c, g) for c in range(8) for g in range(NGRP)], [xokey(c, g) for c in range(8) for g in range(NGRP)])
        u = U_FFN
        for t, J in enumerate(THIRDS):
            for jj, j in enumerate(J):
                ugate, uup = u, u + 1
                for g in range(NGRP):
                    pa = 2 * (g % 2)
                    proj_T(ugate, pa, g)
                    proj_T(uup, pa + 1, g)
                    act(SF[g % 2][:, :], PS[pa][:, :], AF.Silu, [psk(pa)], [sfk(g % 2)])
                    vtt(HID[:, jj, g * G:(g + 1) * G], PS[pa + 1][:, :], SF[g % 2][:, :], ALU.mult,
                        [psk(pa + 1), sfk(g % 2)], [hidkey(jj, g)])
                u += 2
                w_release(u)
            if t < len(THIRDS) - 1:
                for m in range(8):
                    proj_add(u, m, HID, hidkey, len(J), 4)
                    u += 1
                    w_release(u)
            else:
                for g in range(NGRP):
                    for m in range(8):
                        sl, sk = w_slot(u + m)
                        pb = 4 + (m % 2)
                        for c in range(len(J)):
                            mm(PS[pb][:, :], sl[:, c * 128:(c + 1) * 128], HID[:, c, g * G:(g + 1) * G], c == 0, c == len(J) - 1,
                               [sk, hidkey(c, g)], [psk(pb)])
                        vtt(X[:, m, g * G:(g + 1) * G], PS[pb][:, :], X[:, m, g * G:(g + 1) * G], ALU.add,
                            [psk(pb), xkey(m, g)], [xkey(m, g)])
                    if g >= 1:
                        final_norm(g - 1)
                final_norm(NGRP - 1)
                u += 8
                w_release(u)
        assert u == NU

    except _Stop:
        pass
    for nm in ("out", "dbg"):
        if nm in C.dsem:
            d = C.dsem[nm]
            nc.sync.wait_ge(d[0], d[1])
    return nc, dbg_out


def _stat_unit(Wc):
    nk = Wc.shape[0] // 128
    out = np.zeros((128, 1024), np.float32)
    out[:, :nk * 128] = Wc.reshape(nk, 128, 128).transpose(1, 0, 2).reshape(128, nk * 128)
    return out


def _build_units(inp):
    w_in = np.asarray(inp["w_in"], np.float32)[0]
    w_out = np.asarray(inp["w_out"], np.float32)[0]
    w_xq = np.asarray(inp["w_xq"], np.float32)[0]
    w_xkv = np.asarray(inp["w_xkv"], np.float32)[0]
    w_xo = np.asarray(inp["w_xo"], np.float32)[0]
    w_gu = np.asarray(inp["w_gate_up"], np.float32)[0]
    w_dn = np.asarray(inp["w_down"], np.float32)[0]
    wu = np.zeros((NU, 128, 1024), np.float32)
    for m in range(8):
        wu[U_XK + m] = _stat_unit(w_xkv[:, m * 128:(m + 1) * 128])
    for hf in range(2):
        for j in range(4):
            blk = w_xkv[j * 256:(j + 1) * 256, 1024 + hf * 512:1024 + (hf + 1) * 512]
            wu[U_XV + hf * 4 + j] = blk.reshape(2, 128, 512).transpose(1, 0, 2).reshape(128, 1024)
    for hh in range(4):
        cols = np.concatenate([np.arange(hh * 64, hh * 64 + 64), np.arange((4 + hh) * 64, (4 + hh) * 64 + 64)])
        wu[U_QA + hh] = _stat_unit(w_in[:, cols])
    wu[U_KA] = _stat_unit(w_in[:, 512:640])
    wu[U_VA] = _stat_unit(w_in[:, 640:768])
    for hd in range(4):
        wu[U_HG + 4 * hd + 0] = _stat_unit(w_in[:, 1280 + hd * 128:1280 + (hd + 1) * 128])
        wu[U_HG + 4 * hd + 1] = _stat_unit(w_in[:, 768 + hd * 128:768 + (hd + 1) * 128])
        wu[U_HG + 4 * hd + 2] = _stat_unit(w_in[:, 2304 + hd * 128:2304 + (hd + 1) * 128])
        wu[U_HG + 4 * hd + 3] = _stat_unit(w_in[:, 1792 + hd * 128:1792 + (hd + 1) * 128])
    rows = []
    for c in range(4):
        rows.append(np.concatenate([np.arange(c * 64, c * 64 + 64), np.arange((4 + c) * 64, (4 + c) * 64 + 64)]))
    for c in range(4):
        rows.append(512 + c * 128 + np.arange(128))
    rows = np.concatenate(rows)
    w_out_p = w_out[rows, :]
    for m in range(8):
        wu[U_WO + m] = _stat_unit(w_out_p[:, m * 128:(m + 1) * 128])
        wu[U_XQ + m] = _stat_unit(w_xq[:, m * 128:(m + 1) * 128])
        wu[U_XO + m] = _stat_unit(w_xo[:, m * 128:(m + 1) * 128])
    for i, (kind, t, idx) in enumerate(FFN_UNITS):
        u = U_FFN + i
        if kind == "gate":
            wu[u] = _stat_unit(w_gu[:, idx * 128:(idx + 1) * 128])
        elif kind == "up":
            wu[u] = _stat_unit(w_gu[:, 2816 + idx * 128:2816 + (idx + 1) * 128])
        else:
            J = THIRDS[t]
            wu[u] = _stat_unit(w_dn[J[0] * 128:(J[-1] + 1) * 128, idx * 128:(idx + 1) * 128])
    return wu


def _build_prm(inp):
    prm = np.zeros((128, NPRM), np.float32)

    def cols(v):
        return np.asarray(v, np.float32).reshape(-1, 128).T

    prm[:, 0:8] = cols(inp["norm_mix"][0])
    prm[:, 8:16] = cols(inp["norm_xattn"][0])
    prm[:, 16:24] = cols(inp["norm_mem"][0])
    prm[:, 24:32] = cols(inp["norm_ffn"][0])
    prm[:, 32:40] = cols(inp["norm_final"])
    ag = np.asarray(inp["att_out_gain"], np.float32)[0]
    for hh in range(4):
        prm[0:64, 40 + hh] = ag[hh * 64:(hh + 1) * 64]
        prm[64:128, 40 + hh] = ag[(4 + hh) * 64:(5 + hh) * 64]
    prm[:, 44:48] = cols(inp["hg_out_gain"][0])
    lbl = np.asarray(inp["hg_lb_logits"], np.float32)
    prm[:, 48:52] = cols(lbl[0])
    prm[:, 52:56] = cols(lbl[1])
    sinks = np.asarray(inp["att_sinks"], np.float32)[0]
    for hh in range(4):
        prm[0:64, 58 + hh] = sinks[hh]
        prm[64:128, 58 + hh] = sinks[4 + hh]
    half = 8
    inv_freq = np.power(np.float32(500000.0), -np.arange(half, dtype=np.float32) * np.float32(2.0 / 16)).astype(np.float32)
    two_pi = 6.283185
    for p in range(128):
        d = p % 64
        if d < 8:
            prm[p, 56] = inv_freq[d] / np.float32(2 * np.pi)
            prm[p, 57] = -two_pi
        elif d < 16:
            prm[p, 56] = inv_freq[d - 8] / np.float32(2 * np.pi)
            prm[p, 57] = two_pi
    return prm


def _build_cst():
    cst = np.zeros((128, 6, 128), np.float32)
    k = np.arange(128)[:, None]
    q = np.arange(128)[None, :]
    cst[:, 0, :] = (k == q)
    cst[:, 1, :] = 1.0
    dq = q % 64
    cst[:, 2, :] = ((dq < 8) & (k == q + 8)) | ((dq >= 8) & (dq < 16) & (k == q - 8))
    cst[:, 3, :] = (k <= q)
    cst[:, 4, :] = (k > q)
    cst[:, 5, :] = ((k // 64) == (q // 64)) & (k <= q)
    return cst.reshape(128, 768)


_CACHE = {}


def _run(inputs, dbg=None, stop=None):
    key = (tuple(dbg or []), stop)
    if key not in _CACHE:
        _CACHE[key] = build(dbg, stop)
    nc, dbg_out = _CACHE[key]
    x = np.asarray(inputs["x"], np.float32)
    mem = np.asarray(inputs["mem"], np.float32)
    positions = np.asarray(inputs["positions"], np.int32)
    wu = _build_units(inputs)
    prm = _build_prm(inputs)
    cst = _build_cst()
    in_maps = []
    for b in range(8):
        in_maps.append({
            "xT": np.ascontiguousarray(x[b].T),
            "memT": np.ascontiguousarray(mem[b].T),
            "pos": np.ascontiguousarray(np.broadcast_to(positions[b][None, :], (128, S))),
            "wu": wu, "prm": prm, "cst": cst,
        })
    res = run_bass_kernel_spmd(nc, in_maps, core_ids=list(range(8)))
    return res


def kernel(**inputs):
    res = _run(inputs)
    y = np.stack([np.ascontiguousarray(r["yT"].T) for r in res.results], axis=0)
    return y.astype(np.float32)
```
